# Optimizing a Trainium2 kernel written in Bass

```python
import math
import jax, jax.numpy as jnp
from jax import lax
import numpy as np

D_MODEL = 2048
BATCH = 4
SEQ = 2048
DEPTH = 2

HEAD_DIM = 128
ROPE_THETA = 10000.0
NORM_EPS = 1e-6
A_HEADS = 8
A_WIDTH = A_HEADS * HEAD_DIM
DIL_CONFIGS = ((128, 1), (512, 4), (2048, 16))
B_HEADS = 4
B_VDIM = 2 * HEAD_DIM
B_WIDTH = B_HEADS * B_VDIM
Q_BLOCK = 128
C_HEADS = D_MODEL // HEAD_DIM
C_WIDTH = C_HEADS * HEAD_DIM
MOBA_BLOCK = 256
MOBA_TOPK = 3
MOBA_Q_CHUNK = 16
AB_IN = 3 * A_WIDTH + 3 * B_WIDTH + (A_WIDTH + B_WIDTH)
C_IN = 4 * C_WIDTH
N_AB = (DEPTH + 1) // 2
N_C = DEPTH // 2

kernel_name = "hybrid_dilated_diff_moba_gated"


def _rmsnorm(x, g):
    x32 = x.astype(jnp.float32)
    y = x32 * lax.rsqrt(jnp.mean(x32 * x32, axis=-1, keepdims=True) + NORM_EPS)
    return y * g.astype(jnp.float32)


def _rope(t, pos):
    dh = t.shape[-1]
    inv = 1.0 / (ROPE_THETA ** (jnp.arange(0, dh, 2, dtype=jnp.float32) / dh))
    ang = pos.astype(jnp.float32)[:, None] * inv[None, :]
    cos = jnp.concatenate([jnp.cos(ang), jnp.cos(ang)], axis=-1)
    sin = jnp.concatenate([jnp.sin(ang), jnp.sin(ang)], axis=-1)
    x1, x2 = jnp.split(t, 2, axis=-1)
    rot = jnp.concatenate([-x2, x1], axis=-1)
    return t * cos + rot * sin


def _heads(t, n_heads):
    b, s, w = t.shape
    return t.reshape(b, s, n_heads, w // n_heads).transpose(0, 2, 1, 3)


def _merge(t):
    b, h, s, d = t.shape
    return t.transpose(0, 2, 1, 3).reshape(b, s, h * d)


def _dilated_group(q, k, v, dil, nback):
    B, H, S, Dh = q.shape
    blk = nback
    L = -(-S // dil)
    L = -(-L // blk) * blk
    Sp = L * dil
    nb = L // blk
    pad = ((0, 0), (0, 0), (0, Sp - S), (0, 0))

    def split(t):
        t = jnp.pad(t, pad).reshape(B, H, L, dil, Dh).transpose(0, 1, 3, 2, 4)
        return t.reshape(B, H, dil, nb, blk, Dh)

    def with_prev(t):
        prev = jnp.pad(t[:, :, :, :-1], ((0, 0), (0, 0), (0, 0), (1, 0), (0, 0), (0, 0)))
        return jnp.concatenate([prev, t], axis=4)

    qs = split(q)
    kk = with_prev(split(k))
    vv = with_prev(split(v))
    s = jnp.einsum('bhrnid,bhrnjd->bhrnij', qs, kk) * (Dh ** -0.5)
    nbi = jnp.arange(nb)[:, None, None]
    i = jnp.arange(blk)[None, :, None]
    j = jnp.arange(2 * blk)[None, None, :]
    rel = blk + i - j
    kidx = (nbi - 1) * blk + j
    mask = (rel >= 0) & (rel <= nback) & (kidx >= 0)
    s = jnp.where(mask, s, -jnp.inf)
    m = jnp.max(s, axis=-1, keepdims=True)
    p = jnp.exp(s - m)
    den = jnp.sum(p, axis=-1, keepdims=True)
    o = jnp.einsum('bhrnij,bhrnjd->bhrnid', p, vv) / den
    lse = m + jnp.log(den)

    def unsplit(t):
        c = t.shape[-1]
        t = t.reshape(B, H, dil, L, c).transpose(0, 1, 3, 2, 4).reshape(B, H, Sp, c)
        return t[:, :, :S]

    return unsplit(o), unsplit(lse)


def _dilated_mixture(q, k, v):
    outs, lses = [], []
    for window, dil in DIL_CONFIGS:
        o, lse = _dilated_group(q, k, v, dil, window // dil)
        outs.append(o)
        lses.append(lse)
    w = jax.nn.softmax(jnp.stack(lses, axis=0), axis=0)
    return jnp.sum(w * jnp.stack(outs, axis=0), axis=0)


def _diff_attention(q1, q2, k1, k2, v, lam):
    B, H, S, Dh = q1.shape
    scale = Dh ** -0.5
    kpos = jnp.arange(S)

    def blk(c):
        start = c * Q_BLOCK
        qa = lax.dynamic_slice_in_dim(q1, start, Q_BLOCK, axis=2)
        qb = lax.dynamic_slice_in_dim(q2, start, Q_BLOCK, axis=2)
        qpos = start + jnp.arange(Q_BLOCK)
        mask = kpos[None, :] <= qpos[:, None]
        p1 = jax.nn.softmax(jnp.where(mask, jnp.einsum('bhid,bhjd->bhij', qa, k1) * scale, -jnp.inf), axis=-1)
        p2 = jax.nn.softmax(jnp.where(mask, jnp.einsum('bhid,bhjd->bhij', qb, k2) * scale, -jnp.inf), axis=-1)
        return jnp.einsum('bhij,bhjd->bhid', p1 - lam * p2, v)

    o = lax.map(blk, jnp.arange(S // Q_BLOCK))
    return o.transpose(1, 2, 0, 3, 4).reshape(B, H, S, v.shape[-1])


def _moba_attention(q, k, v):
    B, H, S, Dh = q.shape
    scale = Dh ** -0.5
    nb = -(-S // MOBA_BLOCK)
    Sp = nb * MOBA_BLOCK
    pad = ((0, 0), (0, 0), (0, Sp - S), (0, 0))
    q, k, v = jnp.pad(q, pad), jnp.pad(k, pad), jnp.pad(v, pad)
    kb = k.reshape(B, H, nb, MOBA_BLOCK, Dh)
    vb = v.reshape(B, H, nb, MOBA_BLOCK, Dh)
    n_sel = min(MOBA_TOPK, nb - 1)
    if n_sel > 0:
        kmean = jnp.mean(kb, axis=3)
        gate = jnp.einsum('bhsd,bhnd->bhsn', q, kmean)
        qblk = jnp.arange(Sp) // MOBA_BLOCK
        past = jnp.arange(nb)[None, :] < qblk[:, None]
        gate = jnp.where(past, gate, -jnp.inf)
        top_vals, top_idx = lax.top_k(gate, n_sel)
        top_valid = jnp.isfinite(top_vals)
    bi = jnp.arange(B)[:, None, None, None]
    hi = jnp.arange(H)[None, :, None, None]

    def step(c):
        start = c * MOBA_Q_CHUNK
        qc = lax.dynamic_slice_in_dim(q, start, MOBA_Q_CHUNK, axis=2)
        b_own = start // MOBA_BLOCK
        ko = lax.dynamic_index_in_dim(kb, b_own, axis=2, keepdims=False)
        vo = lax.dynamic_index_in_dim(vb, b_own, axis=2, keepdims=False)
        qpos = start + jnp.arange(MOBA_Q_CHUNK)
        kpos = b_own * MOBA_BLOCK + jnp.arange(MOBA_BLOCK)
        s_own = jnp.einsum('bhid,bhjd->bhij', qc, ko) * scale
        s_own = jnp.where(kpos[None, :] <= qpos[:, None], s_own, -jnp.inf)
        if n_sel == 0:
            p = jax.nn.softmax(s_own, axis=-1)
            return jnp.einsum('bhij,bhjd->bhid', p, vo)
        ic = lax.dynamic_slice_in_dim(top_idx, start, MOBA_Q_CHUNK, axis=2)
        vc = lax.dynamic_slice_in_dim(top_valid, start, MOBA_Q_CHUNK, axis=2)
        kg = kb[bi, hi, ic]
        vg = vb[bi, hi, ic]
        s_sel = jnp.einsum('bhid,bhikjd->bhikj', qc, kg) * scale
        s_sel = jnp.where(vc[..., None], s_sel, -jnp.inf)
        s_all = jnp.concatenate([s_own, s_sel.reshape(B, H, MOBA_Q_CHUNK, n_sel * MOBA_BLOCK)], axis=-1)
        p = jax.nn.softmax(s_all, axis=-1)
        p_own = p[..., :MOBA_BLOCK]
        p_sel = p[..., MOBA_BLOCK:].reshape(B, H, MOBA_Q_CHUNK, n_sel, MOBA_BLOCK)
        return jnp.einsum('bhij,bhjd->bhid', p_own, vo) + jnp.einsum('bhikj,bhikjd->bhid', p_sel, vg)

    o = lax.map(step, jnp.arange(Sp // MOBA_Q_CHUNK))
    o = o.transpose(1, 2, 0, 3, 4).reshape(B, H, Sp, Dh)
    return o[:, :, :S]


def _layer_ab(x, norm_g, w_in, w_out, lam_vec, subln_g, layer_idx, pos):
    h = jnp.matmul(_rmsnorm(x, norm_g).astype(x.dtype), w_in).astype(jnp.float32)
    cuts = np.cumsum([A_WIDTH, A_WIDTH, A_WIDTH, B_WIDTH, B_WIDTH, B_WIDTH]).tolist()
    aq, ak, av, bq, bk, bv, z = jnp.split(h, cuts, axis=-1)
    ya = _dilated_mixture(_rope(_heads(aq, A_HEADS), pos), _rope(_heads(ak, A_HEADS), pos), _heads(av, A_HEADS))
    ya = _merge(ya)
    Bn, S = x.shape[0], x.shape[1]
    bq = _rope(bq.reshape(Bn, S, B_HEADS, 2, HEAD_DIM).transpose(0, 2, 3, 1, 4), pos)
    bk = _rope(bk.reshape(Bn, S, B_HEADS, 2, HEAD_DIM).transpose(0, 2, 3, 1, 4), pos)
    bv = _heads(bv, B_HEADS)
    lam_vec = lam_vec.astype(jnp.float32)
    lam_init = 0.8 - 0.6 * math.exp(-0.3 * layer_idx)
    lam = jnp.exp(jnp.sum(lam_vec[0] * lam_vec[1])) - jnp.exp(jnp.sum(lam_vec[2] * lam_vec[3])) + lam_init
    yb = _diff_attention(bq[:, :, 0], bq[:, :, 1], bk[:, :, 0], bk[:, :, 1], bv, lam)
    yb = _rmsnorm(yb, subln_g) * (1.0 - lam_init)
    yb = _merge(yb)
    y = jnp.concatenate([ya, yb], axis=-1) * jax.nn.silu(z)
    return x + jnp.matmul(y.astype(x.dtype), w_out)


def _layer_c(x, norm_g, w_in, w_out, pos):
    h = jnp.matmul(_rmsnorm(x, norm_g).astype(x.dtype), w_in).astype(jnp.float32)
    cq, ck, cv, z = jnp.split(h, [C_WIDTH, 2 * C_WIDTH, 3 * C_WIDTH], axis=-1)
    yc = _moba_attention(_rope(_heads(cq, C_HEADS), pos), _rope(_heads(ck, C_HEADS), pos), _heads(cv, C_HEADS))
    y = _merge(yc) * jax.nn.silu(z)
    return x + jnp.matmul(y.astype(x.dtype), w_out)


def setup_inputs(seed: int = 0) -> dict:
    key = jax.random.key(seed)
    ks = jax.random.split(key, 10)
    f32 = jnp.float32
    x = jax.random.normal(ks[0], (BATCH, SEQ, D_MODEL), f32)
    norm_ab = 1.0 + 0.02 * jax.random.normal(ks[1], (N_AB, D_MODEL), f32)
    w_in_ab = jax.random.normal(ks[2], (N_AB, D_MODEL, AB_IN), f32) * D_MODEL ** -0.5
    w_out_ab = jax.random.normal(ks[3], (N_AB, A_WIDTH + B_WIDTH, D_MODEL), f32) * (A_WIDTH + B_WIDTH) ** -0.5
    lam_ab = 0.1 * jax.random.normal(ks[4], (N_AB, 4, HEAD_DIM), f32)
    subln_ab = 1.0 + 0.02 * jax.random.normal(ks[5], (N_AB, B_VDIM), f32)
    norm_c = 1.0 + 0.02 * jax.random.normal(ks[6], (N_C, D_MODEL), f32)
    w_in_c = jax.random.normal(ks[7], (N_C, D_MODEL, C_IN), f32) * D_MODEL ** -0.5
    w_out_c = jax.random.normal(ks[8], (N_C, C_WIDTH, D_MODEL), f32) * C_WIDTH ** -0.5
    final_norm = 1.0 + 0.02 * jax.random.normal(ks[9], (D_MODEL,), f32)
    return {"x": x, "norm_ab": norm_ab, "w_in_ab": w_in_ab, "w_out_ab": w_out_ab,
            "lam_ab": lam_ab, "subln_ab": subln_ab, "norm_c": norm_c, "w_in_c": w_in_c,
            "w_out_c": w_out_c, "final_norm": final_norm}


def reference(x, norm_ab, w_in_ab, w_out_ab, lam_ab, subln_ab, norm_c, w_in_c, w_out_c, final_norm):
    pos = jnp.arange(x.shape[1], dtype=jnp.int32)
    h = x
    for i in range(DEPTH):
        j = i // 2
        if i % 2 == 0:
            h = _layer_ab(h, norm_ab[j], w_in_ab[j], w_out_ab[j], lam_ab[j], subln_ab[j], i, pos)
        else:
            h = _layer_c(h, norm_c[j], w_in_c[j], w_out_c[j], pos)
    return _rmsnorm(h, final_norm).astype(x.dtype)
```

```python
import math
import os
DBG = set(os.environ.get('KDBG', '').split(','))
from contextlib import ExitStack

import numpy as np
import ml_dtypes

import concourse.bass as bass
import concourse.mybir as mybir
from concourse.bass_utils import run_bass_kernel_spmd

F32 = mybir.dt.float32
BF16 = mybir.dt.bfloat16
ALU = mybir.AluOpType
AF = mybir.ActivationFunctionType
AX = mybir.AxisListType
bf16_np = ml_dtypes.bfloat16

D = 2048
S_LEN = 2048
NT = 16
HD = 128
SCALE = HD ** -0.5
EPS = 1e-6
BIG = 32768.0
LAM_INIT0 = 0.8 - 0.6 * math.exp(-0.3 * 0)


class Res:
    __slots__ = ("w", "r", "name")

    def __init__(self, name=""):
        self.w = None
        self.r = []
        self.name = name


class DmaSlot:
    __slots__ = ("sem", "count", "key", "sw")

    def __init__(self, sem, key):
        self.sem = sem
        self.count = 0
        self.key = key


class Sched:
    ENG = ["pe", "act", "dve", "pool", "sp"]

    def __init__(self, nc, stack):
        self.nc = nc
        self.stack = stack
        self.sems = {}
        self.semobj = {}
        for e in self.ENG:
            s = stack.enter_context(nc.semaphore("s_" + e))
            self.sems[e] = "E" + e
            self.semobj["E" + e] = s
        self.cnt = {e: 0 for e in self.ENG}
        self.prog = {e: [] for e in self.ENG}
        self.seen = {e: {} for e in self.ENG}
        self.slots = []

    def recycle(self):
        self.free = {False: [sl for sl in self.slots if sl.sw is False], True: [sl for sl in self.slots if sl.sw is True]}

    def slot(self, sw=False):
        fr = getattr(self, "free", None)
        if fr and fr[sw]:
            return fr[sw].pop()
        key = "D%d" % (len(self.slots) + 1)
        s = self.stack.enter_context(self.nc.semaphore("d%d" % (len(self.slots) + 1)))
        self.semobj[key] = s
        sl = DmaSlot(s, key)
        sl.sw = sw
        self.slots.append(sl)
        return sl

    def _waits(self, eng, reads, writes):
        need = {}
        for r in reads:
            if r.w is not None:
                k, v = r.w
                need[k] = max(need.get(k, 0), v)
        for w in writes:
            if w.w is not None:
                k, v = w.w
                need[k] = max(need.get(k, 0), v)
            for (k, v) in w.r:
                need[k] = max(need.get(k, 0), v)
        out = []
        for k, v in need.items():
            if eng == "pe" and k == "Epe":
                continue
            if self.seen[eng].get(k, 0) >= v:
                continue
            self.seen[eng][k] = v
            out.append((k, v))
        return out

    def _record(self, tok, reads, writes):
        for r in reads:
            r.r.append(tok)
            if len(r.r) > 64:
                best = {}
                for (k, v) in r.r:
                    best[k] = max(best.get(k, 0), v)
                r.r = list(best.items())
        for w in writes:
            w.w = tok
            w.r = []

    def op(self, eng, fn, reads=(), writes=()):
        waits = self._waits(eng, reads, writes)
        self.cnt[eng] += 1
        tok = (self.sems[eng], self.cnt[eng])
        self._record(tok, reads, writes)
        self.prog[eng].append((waits, fn, (self.sems[eng], 1)))

    def dma(self, eng, fn, slot, reads=(), writes=()):
        waits = self._waits(eng, reads, writes)
        slot.count += 16
        tok = (slot.key, slot.count)
        self._record(tok, reads, writes)
        self.prog[eng].append((waits, fn, (slot.key, 16)))

    def cc(self, fn, reads=(), writes=()):
        sl = self.slot(sw=True) if False else None
        key = "C%d" % (len(self.slots) + 1)
        sem = self.stack.enter_context(self.nc.semaphore("c%d" % (len(self.slots) + 1)))
        self.semobj[key] = sem
        sl = DmaSlot(sem, key)
        sl.sw = None
        self.slots.append(sl)
        waits = self._waits("pool", reads, writes)
        sl.count += 1
        tok = (sl.key, sl.count)
        self._record(tok, reads, writes)
        self.prog["pool"].append((waits, fn, (sl.key, 1)))

    def barrier(self):
        tot = {self.sems[e]: self.cnt[e] for e in self.ENG}
        for sl in self.slots:
            tot[sl.key] = sl.count
        for e in self.ENG:
            waits = []
            for k, v in tot.items():
                if v == 0 or (k == self.sems[e]):
                    continue
                if self.seen[e].get(k, 0) >= v:
                    continue
                self.seen[e][k] = v
                waits.append((k, v))
            own = self.sems[e]
            if self.cnt[e] > 0 and self.seen[e].get(own, 0) < self.cnt[e]:
                self.seen[e][own] = self.cnt[e]
                waits.append((own, self.cnt[e]))
            self.prog[e].append((waits, None, None))

    def emit(self):
        nc = self.nc
        with nc.Block() as block:
            def mk(ename):
                def body(e):
                    for (waits, fn, inc) in self.prog[ename]:
                        for (k, v) in waits:
                            e.wait_ge(self.semobj[k], v)
                        if fn is None:
                            continue
                        ins = fn(e)
                        ins.then_inc(self.semobj[inc[0]], inc[1])
                return body
            block.tensor(mk("pe"))
            block.scalar(mk("act"))
            block.vector(mk("dve"))
            block.gpsimd(mk("pool"))
            block.sync(mk("sp"))


def _tables():
    inv = 1.0 / (10000.0 ** (np.arange(0, HD, 2, dtype=np.float32) / HD))
    ang = np.arange(S_LEN, dtype=np.float32)[:, None] * inv[None, :].astype(np.float32)
    cos = np.cos(ang).astype(np.float32)
    sin = np.sin(ang).astype(np.float32)
    kl = np.arange(128)[:, None]
    ql = np.arange(512)[None, :]

    def mult(d):
        c = (d <= 128).astype(np.float32) + ((d % 4 == 0) & (d <= 512)).astype(np.float32) \
            + (d % 16 == 0).astype(np.float32)
        return np.where(d >= 0, c, 0.0)

    mA = np.stack([mult(ql - kl + 128 * j) for j in range(-3, 6)], 0)
    mC = np.stack([(ql - kl + 128 * j >= 0).astype(np.float32) for j in (0, -1, -2, -3)], 0)
    E = np.zeros((128, 8, 128), np.float32)
    for n in range(8):
        E[n, n, :] = BIG
    return dict(
        cos=cos, sin=sin,
        maskA=np.ascontiguousarray(mA.transpose(1, 0, 2)).astype(bf16_np),
        maskC=np.ascontiguousarray(mC.transpose(1, 0, 2)).astype(bf16_np),
        Emat=E.astype(bf16_np),
        ident=np.eye(128, dtype=np.float32).astype(bf16_np),
    )


class Ctx:
    pass


def _mk_ctx(nc, st):
    K = Ctx()
    K.nc = nc
    K.st = st
    K.S = Sched(nc, st)
    K.banks = [st.enter_context(nc.psum_tensor("pb%d" % i, [128, 512], F32)) for i in range(6)]
    K.r_bank = [Res("bank%d" % i) for i in range(6)]
    K.ptrs = [st.enter_context(nc.psum_tensor("ptr%d" % i, [128, 8, 128], BF16)) for i in range(2)]
    K.r_ptr = [Res("ptrA"), Res("ptrB")]
    K.pm = K.banks[5]
    K.r_pm = K.r_bank[5]
    K.ntr = 0
    return K


def T_(K, st, name, shape, dt):
    K.uid = getattr(K, "uid", 0) + 1
    return st.enter_context(K.nc.sbuf_tensor("sb%d_%s" % (K.uid, name), shape, dt))


def load_const(K, st, name, ap, shape, dt, eng="act", late=None):
    t = T_(K, st, name, shape, dt)
    r = Res(name)
    issue = lambda: K.S.dma(eng if late is None else "sp", lambda e: e.dma_start(out=t[:], in_=ap), K.S.slot(), writes=[r])
    if late is None:
        issue()
    else:
        late.append(issue)
    return t, r


def transpose4(K, srcs, src_res, dst_ap_fn, dst_res, evac_eng, extra_w=()):
    S = K.S
    half = 0 if getattr(K, "single_ptr", False) else K.ntr % 2
    K.ntr += 1
    rp = K.r_ptr[half]
    n = len(srcs)
    for i, sap in enumerate(srcs):
        S.op("pe", lambda e, i=i, sap=sap, idn=K.ident: e.transpose(out=K.ptrs[half][:, i, :], in_=sap, identity=idn[:]),
             reads=list(src_res) + [K.r_ident], writes=[rp])
    pin = K.ptrs[half][:, 0:n, :]
    if evac_eng == "act":
        S.op("act", lambda e: e.copy(out=dst_ap_fn(), in_=pin), reads=[rp], writes=list(dst_res) + list(extra_w))
    else:
        S.op("dve", lambda e: e.tensor_copy(out=dst_ap_fn(), in_=pin), reads=[rp], writes=list(dst_res) + list(extra_w))


def mixer_phase(K, kind, xin, g_ap, w_ap, yT_out, tabs, extra, xnT_src=None, ag=None, pre_cc=None):
    nc, S = K.nc, K.S
    with ExitStack() as st:
        xnT = T_(K, st, "xnT", [128, 16, S_LEN], BF16)
        r_xnT = [Res("xnT%d" % t) for t in range(NT)]
        wsb = [T_(K, st, "wsb%d" % i, [128, 16, 512], BF16) for i in range(2)]
        r_w = [Res("w0"), Res("w1")]
        w_slot = [S.slot(True), S.slot(True)]
        late = []
        K.ident, K.r_ident = load_const(K, st, "ident", tabs["ident"], [128, 128], BF16)
        cos, r_cos = load_const(K, st, "cos", tabs["cos"].rearrange("(t p) d -> p t d", p=128), [128, 16, 64], F32, late=late)
        sin, r_sin = load_const(K, st, "sin", tabs["sin"].rearrange("(t p) d -> p t d", p=128), [128, 16, 64], F32, late=late)
        maskC, r_maskC = load_const(K, st, "maskC", tabs["maskC"], [128, 4, 512], BF16, late=late)
        if kind == "ab":
            maskA, r_maskA = load_const(K, st, "maskA", tabs["maskA"], [128, 9, 512], BF16, late=late)
            gsub, r_gsub = load_const(K, st, "gsub", extra["subln"].partition_broadcast(128), [128, 256], F32)
            lamv = T_(K, st, "lamv", [128, 8], F32)
            r_lamv = Res("lamv")
        else:
            Emat, r_E = load_const(K, st, "Emat", tabs["Emat"], [128, 8, 128], BF16, late=late)
        onesc = T_(K, st, "onesc", [128, 1], BF16)
        r_ones = Res("ones")
        S.op("dve" if "B" in DBG else "pool", lambda e: e.memset(onesc[:], 1.0), writes=[r_ones])

        def load_w(gidx, after=()):
            buf = gidx % 2
            c0 = gidx * 512
            src = w_ap[:, c0:c0 + 512].rearrange("(kc p) n -> p kc n", p=128)
            S.dma("pool", lambda e: e.dma_start(out=wsb[buf][:], in_=src), w_slot[buf], reads=list(after), writes=[r_w[buf]])

        if xnT_src is not None:
            for fn in late:
                fn()
            late = []
            load_w(0)
            if pre_cc:
                issue_ccs(K, pre_cc)
                S.barrier()

        if xnT_src is not None:
            for th in range(2):
                for i4 in range(4):
                    S.dma("sp", lambda e, i4=i4, th=th: e.dma_start(
                        out=xnT[:, i4 * 4:(i4 + 1) * 4, th * 1024:(th + 1) * 1024],
                        in_=xnT_src[th][i4 * 512:(i4 + 1) * 512, :].rearrange("(kc p) t -> p kc t", p=128)),
                        S.slot(), writes=r_xnT[th * 8:(th + 1) * 8])
        if xnT_src is not None:
            for fn in late:
                fn()
            late = []
        with ExitStack() as st1:
          if kind == "ab":
              lamt, r_lamt = load_const(K, st1, "lamt", extra["lam"].partition_broadcast(128), [128, 512], F32)
              lprod = T_(K, st1, "lprod", [128, 256], F32)
              r_lprod = Res("lprod")
              l4 = lamt[:].rearrange("p (a d) -> p a d", d=128)
              lp3 = lprod[:].rearrange("p (a d) -> p a d", d=128)
              S.op("dve", lambda e: e.tensor_tensor(out=lp3, in0=l4[:, 0::2, :], in1=l4[:, 1::2, :], op=ALU.mult),
                   reads=[r_lamt], writes=[r_lprod])
              S.op("dve", lambda e: e.reduce_sum(out=lamv[:, 0:2], in_=lp3, axis=AX.X), reads=[r_lprod], writes=[r_lamv])
              S.op("act", lambda e: e.activation(out=lamv[:, 2:4], in_=lamv[:, 0:2], func=AF.Exp), reads=[r_lamv], writes=[r_lamv])
              S.op("dve", lambda e: e.tensor_scalar(out=lamv[:, 4:5], in0=lamv[:, 3:4], scalar1=lamv[:, 2:3], scalar2=-LAM_INIT0,
                                                    op0=ALU.subtract, op1=ALU.add), reads=[r_lamv], writes=[r_lamv])
              S.op("dve", lambda e: e.tensor_scalar(out=gsub[:], in0=gsub[:], scalar1=1.0 - LAM_INIT0, scalar2=None, op0=ALU.mult),
                   reads=[r_gsub], writes=[r_gsub])

          if xnT_src is None:
              NXB = 4
              xs = [T_(K, st1, "xs%d" % i, [128, D], F32) for i in range(NXB)]
              r_xs = [Res("xs%d" % i) for i in range(NXB)]
              xs_slot = [S.slot() for _ in range(NXB)]
              gt, r_gt = load_const(K, st1, "gt", g_ap.partition_broadcast(128), [128, D], F32)
              sq = T_(K, st1, "sq", [128, D], BF16)
              r_sq = Res("sq")
              xnb = [T_(K, st1, "xnb%d" % i, [128, D], BF16) for i in range(2)]
              r_xnb = [Res("xnb0"), Res("xnb1")]
              stat = T_(K, st1, "stat", [128, 3 * NT], F32)
              r_stat = [Res("stat%d" % t) for t in range(NT)]
              for tt in range(NT):
                  b = tt % 2
                  xb = tt % NXB
                  S.dma("sp", lambda e, tt=tt, xb=xb: e.dma_start(out=xs[xb][:], in_=xin[tt * 128:(tt + 1) * 128, :]),
                        xs_slot[xb], writes=[r_xs[xb]])
                  if tt == 1:
                      load_w(0, after=[r_xs[1]])
                  if tt == 6:
                      for fn in late:
                          fn()
                      late = []
                  ss = stat[:, 3 * tt:3 * tt + 1]
                  ln = stat[:, 3 * tt + 1:3 * tt + 2]
                  rstd = stat[:, 3 * tt + 2:3 * tt + 3]
                  S.op("act", lambda e, xb=xb, ss=ss: e.activation(out=sq[:], in_=xs[xb][:], func=AF.Square, accum_out=ss),
                       reads=[r_xs[xb]], writes=[r_sq, r_stat[tt]])
                  if "A" in DBG:
                      S.op("dve", lambda e, rstd=rstd: e.memset(rstd, 1.0), writes=[r_stat[tt]])
                  else:
                      S.op("act", lambda e, ss=ss, ln=ln: e.activation(out=ln, in_=ss, func=AF.Ln, scale=1.0 / D, bias=K.epsb[:]),
                           reads=[r_stat[tt], K.r_epsb], writes=[r_stat[tt]])
                      S.op("act", lambda e, ln=ln, rstd=rstd: e.activation(out=rstd, in_=ln, func=AF.Exp, scale=-0.5),
                           reads=[r_stat[tt]], writes=[r_stat[tt]])
                  S.op("dve", lambda e, b=b, xb=xb, rstd=rstd: e.scalar_tensor_tensor(out=xnb[b][:], in0=xs[xb][:], scalar=rstd, in1=gt[:],
                                                                               op0=ALU.mult, op1=ALU.mult),
                       reads=[r_xs[xb], r_stat[tt], r_gt], writes=[r_xnb[b]])
                  for q4 in range(4):
                      srcs = [xnb[b][:, (q4 * 4 + i) * 128:(q4 * 4 + i + 1) * 128] for i in range(4)]
                      transpose4(K, srcs, [r_xnb[b]],
                                 lambda tt=tt, q4=q4: xnT[:, q4 * 4:q4 * 4 + 4, tt * 128:(tt + 1) * 128],
                                 [r_xnT[tt]], "act" if q4 % 2 else "dve")
        S.barrier()
        if getattr(K, 'stop', None) == 'norm':
            return

        with ExitStack() as st2:
            qT = T_(K, st2, "qT", [128, 4, S_LEN], BF16)
            kT = T_(K, st2, "kT", [128, 4, S_LEN], BF16)
            r_qT = [Res("qT%d" % t) for t in range(NT)]
            r_kT = [Res("kT%d" % t) for t in range(NT)]
            vbuf = T_(K, st2, "vbuf", [128, NT, 516], BF16)
            r_v = [Res("v%d" % t) for t in range(NT)]
            gz = T_(K, st2, "gz", [128, NT, 512], BF16)
            r_gz = [Res("gz%d" % t) for t in range(NT)]
            t1 = T_(K, st2, "t1", [128, 512], F32)
            t2 = T_(K, st2, "t2", [128, 512], F32)
            r_t1, r_t2 = Res("t1"), Res("t2")
            tok = [T_(K, st2, "tok%d" % i, [128, 512], BF16) for i in range(2)]
            r_tok = [Res("tok0"), Res("tok1")]
            pT = [T_(K, st2, "pT%d" % i, [128, 512], BF16) for i in range(3)]
            r_pT = [Res("pT%d" % i) for i in range(3)]
            ybf = [T_(K, st2, "ybf%d" % i, [128, 512], BF16) for i in range(2)]
            r_ybf = [Res("ybf0"), Res("ybf1")]
            yTs = T_(K, st2, "yTs", [128, 2, S_LEN], BF16)
            r_yTs = [Res("yTs0"), Res("yTs1")]
            y_slot = [S.slot(), S.slot()]
            sm = T_(K, st2, "sm", [128, 64], F32)
            r_sm = Res("sm")
            if kind == "ab":
                bu2 = T_(K, st2, "bu", [128, 2, 256], F32)
                byb2 = T_(K, st2, "byb", [128, 2, 256], F32)
                r_bu2, r_byb2 = [Res("bu0"), Res("bu1")], [Res("byb0"), Res("byb1")]
                btmp = T_(K, st2, "btmp", [128, 256], F32)
                r_btmp = Res("btmp")
            else:
                kms = T_(K, st2, "kms", [128, 64], F32)
                r_kms = Res("kms")
                kmT = T_(K, st2, "kmT", [128, 4, 8], BF16)
                r_kmT = Res("kmT")
                gm = T_(K, st2, "gm", [128, 4, 8, 8], F32)
                r_gm = Res("gm")
                m8 = T_(K, st2, "m8", [128, 4, 8, 8], F32)
                r_m8 = Res("m8")
                pen = T_(K, st2, "pen", [128, 4, NT, 8], BF16)
                r_pen = Res("pen")
                penT = [T_(K, st2, "penT%d" % i, [128, 512], BF16) for i in range(2)]
                r_penT = [Res("penT0"), Res("penT1")]
                for i in range(2):
                    S.op("pool", lambda e, i=i: e.memset(penT[i][:], 0.0), writes=[r_penT[i]])

            bank_rr = [0]
            tail = [None]
            r_ibp = [Res("ibp0"), Res("ibp1")]

            for rnd in range(2):
                rkind = "c" if kind == "c" else ("a" if rnd == 0 else "b")
                if rkind == "b":
                    vv = vbuf[:, :, 0:514].rearrange("p t (h d) -> p t h d", d=257)
                    nvh, dv = 2, 256
                else:
                    vv = vbuf[:, :, 0:516].rearrange("p t (h d) -> p t h d", d=129)
                    nvh, dv = 4, 128
                S.op("pool", lambda e, vv=vv, dv=dv: e.memset(vv[:, :, :, dv:dv + 1], 1.0), writes=list(r_v))

                for gi in range(4):
                    gidx = rnd * 4 + gi
                    if gidx + 1 < 8:
                        load_w(gidx + 1)
                    wb = wsb[gidx % 2]
                    rwb = r_w[gidx % 2]
                    for tt in range(NT):
                        bi = bank_rr[0] % 5
                        bank_rr[0] += 1
                        ps = K.banks[bi]
                        rps = K.r_bank[bi]
                        for kc in range(16):
                            S.op("pe", lambda e, ps=ps, kc=kc, tt=tt, wb=wb: e.matmul(
                                ps[:], lhsT=xnT[:, kc, tt * 128:(tt + 1) * 128], rhs=wb[:, kc, :], start=(kc == 0), stop=(kc == 15)),
                                reads=[r_xnT[tt], rwb], writes=[rps])
                        if tail[0] is not None:
                            tail[0]()
                            tail[0] = None
                        new_tail = None
                        if gi < 2:
                            ps4 = ps[:].rearrange("p (h t d) -> p h t d", h=4, t=2)
                            t14 = t1[:].rearrange("p (h t d) -> p h t d", h=4, t=2)
                            t24 = t2[:].rearrange("p (h t d) -> p h t d", h=4, t=2)
                            tb = tok[tt % 2]
                            rtb = r_tok[tt % 2]
                            tk4 = tb[:].rearrange("p (h t d) -> p h t d", h=4, t=2)
                            cosb = cos[:, tt, :].unsqueeze(1).unsqueeze(1).to_broadcast([128, 4, 2, 64])
                            sinb = sin[:, tt, :].unsqueeze(1).to_broadcast([128, 4, 64])
                            S.op("dve", lambda e, ps4=ps4, t14=t14, cosb=cosb: e.tensor_tensor(out=t14, in0=ps4, in1=cosb, op=ALU.mult),
                                 reads=[rps, r_cos], writes=[r_t1])
                            S.op("dve", lambda e, ps4=ps4, t24=t24, sinb=sinb: e.tensor_tensor(out=t24[:, :, 0, :], in0=ps4[:, :, 1, :], in1=sinb, op=ALU.mult),
                                 reads=[rps, r_sin], writes=[r_t2])
                            S.op("dve", lambda e, ps4=ps4, t24=t24, sinb=sinb: e.tensor_tensor(out=t24[:, :, 1, :], in0=ps4[:, :, 0, :], in1=sinb, op=ALU.mult),
                                 reads=[rps, r_sin], writes=[r_t2])
                            S.op("pool", lambda e, tk4=tk4, t14=t14, t24=t24: e.tensor_tensor(out=tk4[:, :, 0, :], in0=t14[:, :, 0, :], in1=t24[:, :, 0, :], op=ALU.subtract),
                                 reads=[r_t1, r_t2], writes=[rtb])
                            S.op("pool", lambda e, tk4=tk4, t14=t14, t24=t24: e.tensor_tensor(out=tk4[:, :, 1, :], in0=t14[:, :, 1, :], in1=t24[:, :, 1, :], op=ALU.add),
                                 reads=[r_t1, r_t2], writes=[rtb])
                            dstT = qT if gi == 0 else kT
                            rdst = (r_qT if gi == 0 else r_kT)[tt]

                            def pe_tail(dstT=dstT, rdst=rdst, tb=tb, rtb=rtb, tt=tt, gi=gi, rkind=rkind):
                                transpose4(K, [tb[:, h * 128:(h + 1) * 128] for h in range(4)], [rtb],
                                           lambda: dstT[:, 0:4, tt * 128:(tt + 1) * 128], [rdst], "act")
                                if gi == 1 and rkind == "c":
                                    for h in range(4):
                                        S.op("pe", lambda e, h=h: e.matmul(
                                            K.pm[:, h * 16 + tt:h * 16 + tt + 1], lhsT=tb[:, h * 128:(h + 1) * 128], rhs=onesc[:], start=True, stop=True),
                                            reads=[rtb, r_ones], writes=[K.r_pm])
                            new_tail = pe_tail
                        elif gi == 2:
                            psv = ps[:].rearrange("p (h d) -> p h d", d=dv)
                            S.op("act", lambda e, psv=psv, vv=vv, tt=tt, dv=dv: e.copy(out=vv[:, tt, :, 0:dv], in_=psv),
                                 reads=[rps], writes=[r_v[tt]])
                        else:
                            S.op("act", lambda e, ps=ps, tt=tt: e.activation(out=gz[:, tt, :], in_=ps[:], func=AF.Silu),
                                 reads=[rps], writes=[r_gz[tt]])
                        tail[0] = new_tail
                if tail[0] is not None:
                    tail[0]()
                    tail[0] = None

                if getattr(K, 'stop', None) == 'proj%d' % rnd:
                    S.barrier()
                    return
                if rkind == "c":
                    S.op("act", lambda e: e.copy(out=kms[:], in_=K.pm[:, 0:64]), reads=[K.r_pm], writes=[r_kms])
                    k4 = kms[:].rearrange("p (h b t) -> p h b t", h=4, t=2)
                    S.op("dve", lambda e: e.tensor_tensor(out=gm[:, :, 0, :], in0=k4[:, :, :, 0], in1=k4[:, :, :, 1], op=ALU.add),
                         reads=[r_kms], writes=[r_gm])
                    S.op("dve", lambda e: e.tensor_scalar(out=kmT[:], in0=gm[:, :, 0, :], scalar1=1.0 / 256.0, scalar2=None, op0=ALU.mult),
                         reads=[r_gm], writes=[r_kmT])
                    for h in range(4):
                        for t8 in range(8):
                            tt = 8 + t8
                            S.op("pe", lambda e, h=h, t8=t8, tt=tt: e.matmul(
                                K.pm[:, 64 + (h * 8 + t8) * 8:64 + (h * 8 + t8) * 8 + 8], lhsT=qT[:, h, tt * 128:(tt + 1) * 128], rhs=kmT[:, h, :],
                                start=True, stop=True), reads=[r_qT[tt], r_kmT], writes=[K.r_pm])
                    S.op("pool", lambda e: e.memset(gm[:], -1.0e30), reads=[r_kmT], writes=[r_gm])
                    S.op("pool", lambda e: e.memset(pen[:], -1.0), writes=[r_pen])
                    pg = K.pm[:, 64:320].rearrange("p (h t n) -> p h t n", h=4, t=8)
                    for t8 in range(8):
                        qb = (8 + t8) // 2
                        S.op("dve", lambda e, t8=t8, qb=qb: e.tensor_copy(out=gm[:, :, t8, 0:qb], in_=pg[:, :, t8, 0:qb]),
                             reads=[K.r_pm], writes=[r_gm])
                    for h in range(4):
                        for t8 in range(8):
                            S.op("dve", lambda e, h=h, t8=t8: e.max(out=m8[:, h, t8, :], in_=gm[:, h, t8, :]), reads=[r_gm], writes=[r_m8])
                    for h in range(4):
                        for t8 in range(8):
                            S.op("dve", lambda e, h=h, t8=t8: e.tensor_scalar(
                                out=pen[:, h, 8 + t8, :], in0=gm[:, h, t8, :], scalar1=m8[:, h, t8, 2:3], scalar2=1.0,
                                op0=ALU.is_ge, op1=ALU.subtract), reads=[r_gm, r_m8], writes=[r_pen])
                    for tt in range(NT):
                        qb = tt // 2
                        hi = qb + 1 if tt < 8 else None
                        if tt < 8:
                            S.op("pool", lambda e, tt=tt, qb=qb: e.memset(pen[:, :, tt, 0:qb + 1], 0.0), writes=[r_pen])
                        else:
                            S.op("pool", lambda e, tt=tt, qb=qb: e.memset(pen[:, :, tt, qb:qb + 1], 0.0), writes=[r_pen])

                QC = 256 if rkind == "b" else 512
                nqt = QC // 128
                nchunk = S_LEN // QC
                blocks = []
                if rkind == "b":
                    units = [(hb, [(2 * hb + p, hb, p) for p in range(2)]) for hb in range(2)]
                else:
                    units = [(h, [(h, h, 0)]) for h in range(4)]
                for (uh, passes) in units:
                    for c in range(nchunk):
                        nkt = (c + 1) * nqt
                        cblocks = []
                        for (qi, vh, p) in passes:
                            for kt in range(nkt):
                                cblocks.append(dict(uh=uh, c=c, qi=qi, vh=vh, p=p, kt=kt, last=False, first_of_chunk=False))
                        cblocks[0]["first_of_chunk"] = True
                        cblocks[-1]["last"] = True
                        blocks += cblocks

                pend = []
                chunk_ctr = [0]

                def acc_ap(bl, qt):
                    if rkind == "b":
                        return K.banks[2 + 2 * bl["p"] + qt][:, 0:257], K.r_bank[2 + 2 * bl["p"] + qt]
                    par = bl["cpar"]
                    bk = 2 + 2 * par + qt // 2
                    return K.banks[bk][:, 0:258].rearrange("p (a d) -> p a d", d=129)[:, qt % 2, :], K.r_bank[bk]

                def emit_pen_prep(uh, c, slot):
                    srcs = [pen[:, uh, c * 4 + i, :] for i in range(4)]
                    half = 0
                    K.ntr += 1
                    rp = K.r_ptr[half]
                    for i, sap in enumerate(srcs):
                        S.op("pe", lambda e, i=i, sap=sap, half=half, idn=K.ident: e.transpose(out=K.ptrs[half][0:8, i, :], in_=sap, identity=idn[:]),
                             reads=[r_pen, K.r_ident], writes=[rp])
                    S.op("dve", lambda e, half=half, slot=slot: e.tensor_copy(
                        out=penT[slot][0:8, :].rearrange("p (a d) -> p a d", d=128), in_=K.ptrs[half][0:8, 0:4, :]),
                        reads=[rp], writes=[r_penT[slot]])

                sT3 = K.ptrs[1][:].rearrange("p a b -> p (a b)").bitcast(F32)
                sbanks = [(K.banks[0][:], K.r_bank[0]), (K.banks[1][:], K.r_bank[1]), (sT3, K.r_ptr[1])]

                def emit_qk(i):
                    bl = blocks[i]
                    ps, rps = sbanks[i % 3]
                    c, kt, qi = bl["c"], bl["kt"], bl["qi"]
                    if bl["first_of_chunk"]:
                        bl["cidx"] = chunk_ctr[0]
                        chunk_ctr[0] += 1
                        if rkind == "c" and c >= 2:
                            emit_pen_prep(bl["uh"], c, bl["cidx"] % 2)
                    else:
                        bl["cidx"] = blocks[i - 1]["cidx"]
                    bl["cpar"] = bl["cidx"] % 2
                    use_pen = (rkind == "c") and c >= 2 and (kt // 2 < (c * 4 + 3) // 2)
                    S.op("pe", lambda e, ps=ps, qi=qi, kt=kt, c=c, use_pen=use_pen, QC=QC: e.matmul(
                        ps[:, 0:QC], lhsT=kT[:, qi, kt * 128:(kt + 1) * 128], rhs=qT[:, qi, c * QC:(c + 1) * QC],
                        start=True, stop=(not use_pen)),
                        reads=[r_kT[kt]] + [r_qT[c * nqt + j] for j in range(nqt)], writes=[rps])
                    if use_pen:
                        slot = bl["cpar"]
                        S.op("pe", lambda e, ps=ps, kt=kt, slot=slot, QC=QC: e.matmul(
                            ps[:, 0:QC], lhsT=Emat[:, kt // 2, :], rhs=penT[slot][:], start=False, stop=True),
                            reads=[r_E, r_penT[slot]], writes=[rps])

                def emit_rest(i):
                    bl = blocks[i]
                    ps, rps = sbanks[i % 3]
                    pb = i % 3
                    c, kt, vh = bl["c"], bl["kt"], bl["vh"]
                    S.op("act", lambda e, ps=ps, pb=pb, QC=QC: e.activation(out=pT[pb][:, 0:QC], in_=ps[:, 0:QC], func=AF.Exp, scale=SCALE),
                         reads=[rps], writes=[r_pT[pb]])
                    j = c * nqt - kt
                    if rkind == "a":
                        mk = maskA[:, min(j, 5) + 3, :]
                        rmk = r_maskA
                    elif j <= 0:
                        mk = maskC[:, -j, 0:QC]
                        rmk = r_maskC
                    else:
                        mk = None
                    if mk is not None:
                        S.op("dve", lambda e, pb=pb, mk=mk, QC=QC: e.tensor_tensor(out=pT[pb][:, 0:QC], in0=pT[pb][:, 0:QC], in1=mk, op=ALU.mult),
                             reads=[rmk], writes=[r_pT[pb]])
                    for qt in range(nqt):
                        if kt > c * nqt + qt:
                            continue
                        acc, racc = acc_ap(bl, qt)
                        if rkind == "b":
                            st_flag = (kt == 0)
                        else:
                            st_flag = (kt == 0 and qt % 2 == 0)
                        sp_flag = (kt == c * nqt + qt)
                        S.op("pe", lambda e, acc=acc, pb=pb, qt=qt, kt=kt, vh=vh, st_flag=st_flag, sp_flag=sp_flag, vv=vv: e.matmul(
                            acc, lhsT=pT[pb][:, qt * 128:(qt + 1) * 128], rhs=vv[:, kt, vh, :], start=st_flag, stop=sp_flag, skip_group_check=True),
                            reads=[r_pT[pb], r_v[kt]], writes=[racc])
                    if bl["last"]:
                        finalize(i, bl)

                def finalize(i, bl):
                    c, uh = bl["c"], bl["uh"]
                    yb_i = bl["cidx"] % 2
                    yb_t = ybf[yb_i]
                    r_yb = r_ybf[yb_i]
                    if rkind != "b":
                        for qt in range(4):
                            tt = c * 4 + qt
                            acc, racc = acc_ap(bl, qt)
                            rc = sm[:, qt:qt + 1]
                            S.op("dve", lambda e, acc=acc, rc=rc: e.reciprocal(out=rc, in_=acc[:, 128:129]), reads=[racc], writes=[r_sm])
                            S.op("dve", lambda e, acc=acc, rc=rc, qt=qt, tt=tt, uh=uh, yb_t=yb_t: e.scalar_tensor_tensor(
                                out=yb_t[:, qt * 128:(qt + 1) * 128], in0=acc[:, 0:128], scalar=rc, in1=gz[:, tt, uh * 128:(uh + 1) * 128],
                                op0=ALU.mult, op1=ALU.mult), reads=[racc, r_sm, r_gz[tt]], writes=[r_yb])
                        ysl = uh % 2

                        def tr(c=c, uh=uh, yb_t=yb_t, r_yb=r_yb, ysl=ysl):
                            transpose4(K, [yb_t[:, q * 128:(q + 1) * 128] for q in range(4)], [r_yb],
                                       lambda: yTs[:, ysl, c * 512:(c + 1) * 512].rearrange("p (a d) -> p a d", d=128),
                                       [r_yTs[ysl]], "dve")
                            if c == nchunk - 1:
                                row0 = rnd * 512 + uh * 128
                                S.dma("sp", lambda e: e.dma_start(out=yT_out.get(row0, 128), in_=yTs[:, ysl, :]),
                                      y_slot[ysl], reads=[r_yTs[ysl]], writes=[r_ibp[row0 // 512]])
                        pend.append((i + 2, tr))
                    else:
                        for ph in range(3):
                          for qt in range(2):
                            tt = c * 2 + qt
                            a1, ra1 = K.banks[2 + qt][:, 0:257], K.r_bank[2 + qt]
                            a2, ra2 = K.banks[4 + qt][:, 0:257], K.r_bank[4 + qt]
                            bu, byb, r_bu, r_byb = bu2[:, qt, :], byb2[:, qt, :], r_bu2[qt], r_byb2[qt]
                            o = 8 * qt
                            r1 = sm[:, o:o + 1]
                            r2 = sm[:, o + 1:o + 2]
                            ssq = sm[:, o + 2:o + 3]
                            lnv = sm[:, o + 3:o + 4]
                            rstd = sm[:, o + 4:o + 5]
                            if ph == 0:
                                S.op("dve", lambda e, a1=a1, r1=r1: e.reciprocal(out=r1, in_=a1[:, 256:257]), reads=[ra1], writes=[r_sm])
                                S.op("dve", lambda e, a1=a1, r1=r1, bu=bu: e.tensor_scalar(out=bu, in0=a1[:, 0:256], scalar1=r1, scalar2=None, op0=ALU.mult),
                                     reads=[ra1, r_sm], writes=[r_bu])
                                continue
                            if ph == 1:
                                S.op("dve", lambda e, a2=a2, r2=r2: e.reciprocal(out=r2, in_=a2[:, 256:257]), reads=[ra2], writes=[r_sm])
                                S.op("dve", lambda e, r2=r2: e.tensor_tensor(out=r2, in0=r2, in1=lamv[:, 4:5], op=ALU.mult), reads=[r_lamv], writes=[r_sm])
                                S.op("dve", lambda e, a2=a2, r2=r2, bu=bu, byb=byb: e.scalar_tensor_tensor(out=byb, in0=a2[:, 0:256], scalar=r2, in1=bu,
                                                                                                  op0=ALU.mult, op1=ALU.add),
                                     reads=[ra2, r_sm, r_bu], writes=[r_byb])
                                continue
                            S.op("dve", lambda e, ssq=ssq, byb=byb, bu=bu: e.scalar_tensor_tensor(out=bu, in0=byb, scalar=1.0, in1=byb,
                                                                                  op0=ALU.mult, op1=ALU.mult, accum_out=ssq),
                                 reads=[r_byb], writes=[r_bu, r_sm])

                        def stage2(c=c, uh=uh, yb_t=yb_t, r_yb=r_yb):
                            for qt in range(2):
                                tt = c * 2 + qt
                                byb, r_byb = byb2[:, qt, :], r_byb2[qt]
                                o = 8 * qt
                                ssq = sm[:, o + 2:o + 3]
                                lnv = sm[:, o + 3:o + 4]
                                rstd = sm[:, o + 4:o + 5]
                                S.op("act", lambda e, ssq=ssq, lnv=lnv: e.activation(out=lnv, in_=ssq, func=AF.Ln, scale=1.0 / 256.0, bias=K.epsb[:]),
                                     reads=[r_sm, K.r_epsb], writes=[r_sm])
                                S.op("act", lambda e, lnv=lnv, rstd=rstd: e.activation(out=rstd, in_=lnv, func=AF.Exp, scale=-0.5), reads=[r_sm], writes=[r_sm])
                                S.op("pool", lambda e, rstd=rstd, byb=byb: e.tensor_scalar(out=btmp[:], in0=byb, scalar1=rstd, scalar2=1.0,
                                                                                       op0=ALU.mult, op1=ALU.mult),
                                     reads=[r_byb, r_sm], writes=[r_btmp])
                                S.op("pool", lambda e: e.tensor_tensor(out=btmp[:], in0=btmp[:], in1=gsub[:], op=ALU.mult),
                                     reads=[r_gsub], writes=[r_btmp])
                                S.op("pool", lambda e, qt=qt, tt=tt: e.tensor_tensor(
                                    out=yb_t[:, qt * 256:(qt + 1) * 256], in0=btmp[:], in1=gz[:, tt, uh * 256:(uh + 1) * 256], op=ALU.mult),
                                    reads=[r_btmp, r_gz[tt]], writes=[r_yb])
                            pend.append((stage2.due3, trb))

                        def trb(c=c, uh=uh, yb_t=yb_t, r_yb=r_yb):
                            half = 0
                            K.ntr += 1
                            rp = K.r_ptr[half]
                            for i4 in range(4):
                                S.op("pe", lambda e, i4=i4, half=half, idn=K.ident: e.transpose(out=K.ptrs[half][:, i4, :], in_=yb_t[:, i4 * 128:(i4 + 1) * 128],
                                                                                   identity=idn[:]), reads=[r_yb, K.r_ident], writes=[rp])
                            for fh in range(2):
                                S.op("dve", lambda e, fh=fh, half=half: e.tensor_copy(
                                    out=yTs[:, fh, c * 256:(c + 1) * 256].rearrange("p (a d) -> p a d", d=128),
                                    in_=K.ptrs[half][:, fh:4:2, :]), reads=[rp], writes=[r_yTs[fh]])
                            if c == nchunk - 1:
                                for fh in range(2):
                                    row0 = rnd * 512 + uh * 256 + fh * 128
                                    S.dma("sp", lambda e, fh=fh, row0=row0: e.dma_start(out=yT_out.get(row0, 128), in_=yTs[:, fh, :]),
                                          y_slot[fh], reads=[r_yTs[fh]], writes=[r_ibp[row0 // 512]])
                        nxt = 0
                        j2 = i + 1
                        while j2 < len(blocks):
                            nxt += 1
                            if blocks[j2]["last"]:
                                break
                            j2 += 1
                        k1 = max(2, min(3, nxt - 1))
                        stage2.due3 = i + k1 + 4
                        pend.append((i + k1, stage2))

                nb = len(blocks)
                K.single_ptr = True
                emit_qk(0)
                emit_qk(1)
                for i in range(nb):
                    if i + 2 < nb:
                        emit_qk(i + 2)
                    due = [p for p in pend if p[0] <= i]
                    pend[:] = [p for p in pend if p[0] > i]
                    for (_, fn) in due:
                        fn()
                    emit_rest(i)
                for (_, fn) in pend:
                    fn()
                pend[:] = []
                K.single_ptr = False
                if ag is not None and rnd == 0:
                    src0, dst0 = ag[0].pieces[0], ag[1].pieces[0]
                    S.cc(lambda e: e.collective_compute("AllGather", ALU.bypass, replica_groups=RG_PAIRS, ins=[src0], outs=[dst0]),
                         reads=[r_ibp[0]])
        S.barrier()
        S.recycle()


def outproj_phase(K, final, yT_in, res_in, w_ap, out_ap, tabs, g_ap=None):
    nc, S = K.nc, K.S
    with ExitStack() as st:
        yTi = T_(K, st, "yTi", [128, 16, 1024], BF16)
        r_yTi = [Res("yTi%d" % i) for i in range(4)]
        for i in range(4):
            S.dma("sp", lambda e, i=i: e.dma_start(out=yTi[:, i * 4:(i + 1) * 4, :],
                                                   in_=yT_in[i * 512:(i + 1) * 512, :].rearrange("(kc p) t -> p kc t", p=128)),
                  S.slot(), writes=[r_yTi[i]])
        wo = T_(K, st, "wo", [128, 4, 16, 512], BF16)
        r_wo = [Res("wo%d" % i) for i in range(4)]
        for cg in range(4):
            S.dma("pool", lambda e, cg=cg: e.dma_start(out=wo[:, cg, :, :],
                                                       in_=w_ap[:, cg * 512:(cg + 1) * 512].rearrange("(kc p) n -> p kc n", p=128)),
                  S.slot(True), writes=[r_wo[cg]])
        res = T_(K, st, "res", [128, 8, D], F32)
        r_res = [Res("res%d" % i) for i in range(8)]
        for tt in range(8):
            S.dma("sp", lambda e, tt=tt: e.dma_start(out=res[:, tt, :], in_=res_in[tt * 128:(tt + 1) * 128, :]), S.slot(), writes=[r_res[tt]])
        o_slot = [S.slot(), S.slot()]
        if final:
            gt, r_gt = load_const(K, st, "gfin", g_ap.partition_broadcast(128), [128, D], F32)
            sq = T_(K, st, "sq2", [128, D], BF16)
            r_sq = Res("sq2")
            ob = [T_(K, st, "ob%d" % i, [128, D], F32) for i in range(2)]
            r_ob = [Res("ob0"), Res("ob1")]
            stat = T_(K, st, "stat2", [128, 24], F32)
            r_stat = Res("stat2")
        cnt = 0
        for tt in range(8):
            for cg in range(4):
                bi = cnt % 6
                cnt += 1
                ps, rps = K.banks[bi], K.r_bank[bi]
                for kc in range(16):
                    S.op("pe", lambda e, ps=ps, kc=kc, tt=tt, cg=cg: e.matmul(
                        ps[:], lhsT=yTi[:, kc, tt * 128:(tt + 1) * 128], rhs=wo[:, cg, kc, :], start=(kc == 0), stop=(kc == 15)),
                        reads=[r_yTi[kc // 4], r_wo[cg]], writes=[rps])
                S.op("dve", lambda e, ps=ps, tt=tt, cg=cg: e.tensor_tensor(out=res[:, tt, cg * 512:(cg + 1) * 512], in0=ps[:],
                                                                           in1=res[:, tt, cg * 512:(cg + 1) * 512], op=ALU.add),
                     reads=[rps], writes=[r_res[tt]])
            if not final:
                S.dma("sp", lambda e, tt=tt: e.dma_start(out=out_ap[tt * 128:(tt + 1) * 128, :], in_=res[:, tt, :]), o_slot[tt % 2], reads=[r_res[tt]])
            else:
                ss = stat[:, 3 * tt:3 * tt + 1]
                ln = stat[:, 3 * tt + 1:3 * tt + 2]
                rstd = stat[:, 3 * tt + 2:3 * tt + 3]
                b = tt % 2
                S.op("act", lambda e, tt=tt, ss=ss: e.activation(out=sq[:], in_=res[:, tt, :], func=AF.Square, accum_out=ss),
                     reads=[r_res[tt]], writes=[r_sq, r_stat])
                S.op("act", lambda e, ss=ss, ln=ln: e.activation(out=ln, in_=ss, func=AF.Ln, scale=1.0 / D, bias=K.epsb[:]),
                     reads=[r_stat, K.r_epsb], writes=[r_stat])
                S.op("act", lambda e, ln=ln, rstd=rstd: e.activation(out=rstd, in_=ln, func=AF.Exp, scale=-0.5), reads=[r_stat], writes=[r_stat])
                S.op("dve", lambda e, tt=tt, b=b, rstd=rstd: e.scalar_tensor_tensor(out=ob[b][:], in0=res[:, tt, :], scalar=rstd, in1=gt[:],
                                                                                   op0=ALU.mult, op1=ALU.mult),
                     reads=[r_res[tt], r_stat, r_gt], writes=[r_ob[b]])
                S.dma("sp", lambda e, tt=tt, b=b: e.dma_start(out=out_ap[tt * 128:(tt + 1) * 128, :], in_=ob[b][:]), o_slot[b], reads=[r_ob[b]])
    S.barrier()


RG_PAIRS = [[0, 1], [2, 3], [4, 5], [6, 7]]


class Rows:
    def __init__(self, pieces, P):
        self.pieces = pieces
        self.P = P

    def get(self, a, n):
        assert a // self.P == (a + n - 1) // self.P
        return self.pieces[a // self.P][a % self.P:a % self.P + n, :]


def issue_ccs(K, pairs):
    for (sp_, dp_) in pairs:
        K.S.cc(lambda e, sp_=sp_, dp_=dp_: e.collective_compute("AllGather", ALU.bypass, replica_groups=RG_PAIRS, ins=[sp_], outs=[dp_]))


def collective_ag_rows(K, src, dst, pieces=(0, 1)):
    S = K.S
    S.barrier()
    for sp_, dp_ in [(src.pieces[i], dst.pieces[i]) for i in pieces]:
        S.op("pool", lambda e, sp_=sp_, dp_=dp_: e.collective_compute("AllGather", ALU.bypass, replica_groups=RG_PAIRS, ins=[sp_], outs=[dp_]))
    S.barrier()


def collective_ag(K, src_ap, dst_ap):
    S = K.S
    S.barrier()
    S.op("pool", lambda e: e.collective_compute("AllGather", ALU.bypass, replica_groups=RG_PAIRS, ins=[src_ap], outs=[dst_ap]))
    S.barrier()


def outproj_cols(K, G, w_ap, res_ap, g_ap, final, out_ap, hd_ap, ss_ib, ss_ob, pre_cc=None):
    nc, S = K.nc, K.S
    with ExitStack() as st:
        hown = T_(K, st, "hown", [128, NT, 1024], F32)
        r_h = [Res("h%d" % t) for t in range(NT)]
        hsl = [S.slot() for _ in range(NT)]
        for tt in range(NT):
            S.dma("sp", lambda e, tt=tt: e.dma_start(out=hown[:, tt, :], in_=res_ap[tt * 128:(tt + 1) * 128, :]), hsl[tt], writes=[r_h[tt]])
        stat = T_(K, st, "ostat", [128, 64], F32)
        r_stat = Res("ostat")
        gt, r_gt = load_const(K, st, "og", g_ap.partition_broadcast(128), [128, 1024], F32)
        K.ident, K.r_ident = load_const(K, st, "oident", K.tabs["ident"], [128, 128], BF16)
        with ExitStack() as st1:
            yTi = T_(K, st1, "yTi", [128, 16, S_LEN], BF16)
            r_yTi = [Res("yTi%d" % i) for i in range(4)]
            wo = T_(K, st1, "wo", [128, 2, 16, 512], BF16)
            r_wo = [Res("wo0"), Res("wo1")]
            if pre_cc:
                issue_ccs(K, pre_cc)
            for cg in range(2):
                S.dma("pool", lambda e, cg=cg: e.dma_start(out=wo[:, cg, :, :],
                                                           in_=w_ap[:, cg * 512:(cg + 1) * 512].rearrange("(kc p) n -> p kc n", p=128)),
                      S.slot(True), writes=[r_wo[cg]])
            if pre_cc:
                S.barrier()
            for i in range(4):
                S.dma("sp", lambda e, i=i: e.dma_start(out=yTi[:, i * 4:(i + 1) * 4, :],
                                                       in_=G.get(i * 512, 512).rearrange("(kc p) t -> p kc t", p=128)),
                      S.slot(), writes=[r_yTi[i]])
            sq = T_(K, st1, "osq", [128, 1024], BF16)
            r_sq = Res("osq")
            cnt = 0
            for tt in range(NT):
                for cg in range(2):
                    bi = cnt % 6
                    cnt += 1
                    ps, rps = K.banks[bi], K.r_bank[bi]
                    for kc in range(16):
                        S.op("pe", lambda e, ps=ps, kc=kc, tt=tt, cg=cg: e.matmul(
                            ps[:], lhsT=yTi[:, kc, tt * 128:(tt + 1) * 128], rhs=wo[:, cg, kc, :], start=(kc == 0), stop=(kc == 15)),
                            reads=[r_yTi[kc // 4], r_wo[cg]], writes=[rps])
                    S.op("dve", lambda e, ps=ps, tt=tt, cg=cg: e.tensor_tensor(out=hown[:, tt, cg * 512:(cg + 1) * 512], in0=ps[:],
                                                                               in1=hown[:, tt, cg * 512:(cg + 1) * 512], op=ALU.add),
                         reads=[rps], writes=[r_h[tt]])
                S.op("act", lambda e, tt=tt: e.activation(out=sq[:], in_=hown[:, tt, :], func=AF.Square, accum_out=stat[:, tt:tt + 1]),
                     reads=[r_h[tt]], writes=[r_sq, r_stat])
                if not final:
                    S.dma("sp", lambda e, tt=tt: e.dma_start(out=hd_ap[tt * 128:(tt + 1) * 128, :], in_=hown[:, tt, :]), hsl[tt], reads=[r_h[tt]])
        S.dma("sp", lambda e: e.dma_start(out=ss_ib, in_=stat[:, 0:16]), S.slot(), reads=[r_stat])
        collective_ag(K, ss_ib, ss_ob)
        S.dma("sp", lambda e: e.dma_start(out=stat[:, 16:48].rearrange("p (r t) -> p r t", r=2), in_=ss_ob.rearrange("(r p) t -> p r t", p=128)),
              S.slot(), writes=[r_stat])
        S.op("dve", lambda e: e.tensor_tensor(out=stat[:, 48:64], in0=stat[:, 16:32], in1=stat[:, 32:48], op=ALU.add), reads=[r_stat], writes=[r_stat])
        S.op("act", lambda e: e.activation(out=stat[:, 48:64], in_=stat[:, 48:64], func=AF.Ln, scale=1.0 / D, bias=K.epsb[:]),
             reads=[r_stat, K.r_epsb], writes=[r_stat])
        S.op("act", lambda e: e.activation(out=stat[:, 48:64], in_=stat[:, 48:64], func=AF.Exp, scale=-0.5), reads=[r_stat], writes=[r_stat])
        with ExitStack() as st2:
            if final:
                ob = [T_(K, st2, "ob%d" % i, [128, 1024], F32) for i in range(2)]
                r_ob = [Res("ob0"), Res("ob1")]
                o_slot = [S.slot(), S.slot()]
                for tt in range(NT):
                    b = tt % 2
                    S.op("dve", lambda e, tt=tt, b=b: e.scalar_tensor_tensor(out=ob[b][:], in0=hown[:, tt, :], scalar=stat[:, 48 + tt:49 + tt], in1=gt[:],
                                                                            op0=ALU.mult, op1=ALU.mult),
                         reads=[r_h[tt], r_stat, r_gt], writes=[r_ob[b]])
                    S.dma("sp", lambda e, tt=tt, b=b: e.dma_start(out=out_ap[tt * 128:(tt + 1) * 128, :], in_=ob[b][:]), o_slot[b], reads=[r_ob[b]])
            else:
                xnb = [T_(K, st2, "oxnb%d" % i, [128, 1024], BF16) for i in range(2)]
                r_xnb = [Res("oxnb0"), Res("oxnb1")]
                xnTo = T_(K, st2, "xnTo", [128, 8, S_LEN], BF16)
                r_xo = [Res("xnTo0"), Res("xnTo1")]
                for tt in range(NT):
                    b = tt % 2
                    S.op("dve", lambda e, tt=tt, b=b: e.scalar_tensor_tensor(out=xnb[b][:], in0=hown[:, tt, :], scalar=stat[:, 48 + tt:49 + tt], in1=gt[:],
                                                                            op0=ALU.mult, op1=ALU.mult),
                         reads=[r_h[tt], r_stat, r_gt], writes=[r_xnb[b]])
                    for q4 in range(2):
                        srcs = [xnb[b][:, (q4 * 4 + i) * 128:(q4 * 4 + i + 1) * 128] for i in range(4)]
                        transpose4(K, srcs, [r_xnb[b]], lambda tt=tt, q4=q4: xnTo[:, q4 * 4:q4 * 4 + 4, tt * 128:(tt + 1) * 128],
                                   [r_xo[tt // 8]], "act" if q4 % 2 else "dve")
                    if tt % 8 == 7:
                        th = tt // 8
                        r_piece = Res("ib2p%d" % th)
                        S.dma("sp", lambda e, th=th: e.dma_start(out=out_ap[0][th].rearrange("(kc p) t -> p kc t", p=128),
                                                                 in_=xnTo[:, :, th * 1024:(th + 1) * 1024]), S.slot(), reads=[r_xo[th]], writes=[r_piece])
                        if th == 0:
                            src0, dst0 = out_ap[0][0], out_ap[1][0]
                            S.cc(lambda e: e.collective_compute("AllGather", ALU.bypass, replica_groups=RG_PAIRS, ins=[src0], outs=[dst0]),
                                 reads=[r_piece])
    S.barrier()
    S.recycle()


def build_fused():
    nc = bass.Bass("TRN2", target_bir_lowering=False)
    xin = _dram_in(nc, "xin", [S_LEN, D], F32)
    xres = _dram_in(nc, "xres", [S_LEN, 1024], F32)
    g0 = _dram_in(nc, "g0", [1, D], F32)
    w0 = _dram_in(nc, "w0", [D, 4096], F32)
    lam = _dram_in(nc, "lam", [1, 512], F32)
    subln = _dram_in(nc, "subln", [1, 256], F32)
    wo0 = _dram_in(nc, "wo0", [D, 1024], F32)
    gc = _dram_in(nc, "gc", [1, 1024], F32)
    w1 = _dram_in(nc, "w1", [D, 4096], F32)
    wo1 = _dram_in(nc, "wo1", [D, 1024], F32)
    gf = _dram_in(nc, "gf", [1, 1024], F32)
    tabs = _tab_aps(nc)
    out = _dram_out(nc, "out", [S_LEN, 1024], F32)
    def mk_pair(tag):
        ib = Rows([nc.dram_tensor("ib%s_%d" % (tag, i), [512, S_LEN], BF16).ap() for i in range(2)], 512)
        G = Rows([nc.dram_tensor("G%s_%d" % (tag, i), [1024, S_LEN], BF16).ap() for i in range(2)], 1024)
        return ib, G
    ib1, G1 = mk_pair("1")
    ib2 = [nc.dram_tensor("ib2_%d" % i, [1024, 1024], BF16).ap() for i in range(2)]
    G2 = [nc.dram_tensor("G2_%d" % i, [2048, 1024], BF16).ap() for i in range(2)]
    ib3, G3 = mk_pair("3")
    hd = nc.dram_tensor("hd", [S_LEN, 1024], F32).ap()
    ssi = [nc.dram_tensor("ssi%d" % i, [128, 16], F32).ap() for i in range(2)]
    sso = [nc.dram_tensor("sso%d" % i, [256, 16], F32).ap() for i in range(2)]
    with ExitStack() as st:
        K = _mk_ctx(nc, st)
        K.tabs = tabs
        _common(K, st)
        mixer_phase(K, "ab", xin, g0, w0, ib1, tabs, dict(lam=lam, subln=subln), ag=(ib1, G1))
        outproj_cols(K, G1, wo0, xres, gc, False, (ib2, G2), hd, ssi[0], sso[0], pre_cc=[(ib1.pieces[1], G1.pieces[1])])
        mixer_phase(K, "c", None, None, w1, ib3, tabs, {}, xnT_src=G2, ag=(ib3, G3), pre_cc=[(ib2[1], G2[1])])
        outproj_cols(K, G3, wo1, hd, gf, True, out, None, ssi[1], sso[1], pre_cc=[(ib3.pieces[1], G3.pieces[1])])
        K.S.emit()
    return nc


def _common(K, st):
    K.epsb = T_(K, st, "epsb", [128, 1], F32)
    K.r_epsb = Res("epsb")
    K.S.op("dve" if "B" in DBG else "pool", lambda e: e.memset(K.epsb[:], EPS), writes=[K.r_epsb])


def _dram_in(nc, name, shape, dt):
    return nc.dram_tensor(name, list(shape), dt, kind="ExternalInput").ap()


def _dram_out(nc, name, shape, dt):
    return nc.dram_tensor(name, list(shape), dt, kind="ExternalOutput").ap()


def _tab_aps(nc):
    return dict(
        cos=_dram_in(nc, "cos", [S_LEN, 64], F32), sin=_dram_in(nc, "sin", [S_LEN, 64], F32),
        maskA=_dram_in(nc, "maskA", [128, 9, 512], BF16), maskC=_dram_in(nc, "maskC", [128, 4, 512], BF16),
        Emat=_dram_in(nc, "Emat", [128, 8, 128], BF16), ident=_dram_in(nc, "ident", [128, 128], BF16))


def build_mixer(kind, stop=None):
    nc = bass.Bass("TRN2", target_bir_lowering=False)
    xin = _dram_in(nc, "xin", [S_LEN, D], F32)
    g = _dram_in(nc, "g", [1, D], F32)
    w = _dram_in(nc, "w", [D, 4096], F32)
    tabs = _tab_aps(nc)
    extra = {}
    if kind == "ab":
        extra["lam"] = _dram_in(nc, "lam", [1, 512], F32)
        extra["subln"] = _dram_in(nc, "subln", [1, 256], F32)
    yT = _dram_out(nc, "yT", [1024, S_LEN], BF16)
    with ExitStack() as st:
        K = _mk_ctx(nc, st)
        K.stop = stop
        _common(K, st)
        mixer_phase(K, kind, xin, g, w, yT, tabs, extra)
        K.S.emit()
    return nc


def build_outproj(final):
    nc = bass.Bass("TRN2", target_bir_lowering=False)
    yT = _dram_in(nc, "yTin", [D, 1024], BF16)
    res = _dram_in(nc, "res", [1024, D], F32)
    w = _dram_in(nc, "w", [D, D], F32)
    g = _dram_in(nc, "g", [1, D], F32) if final else None
    out = _dram_out(nc, "out", [1024, D], F32)
    with ExitStack() as st:
        K = _mk_ctx(nc, st)
        _common(K, st)
        outproj_phase(K, final, yT, res, w, out, None, g)
        K.S.emit()
    return nc


def _cols_ab(hh):
    r = lambda a, n: list(range(a, a + n))
    o = hh * 512
    return (r(0 + o, 512) + r(1024 + o, 512) + r(2048 + o, 512) + r(6144 + o, 512)
            + r(3072 + o, 512) + r(4096 + o, 512) + r(5120 + o, 512) + r(6144 + 1024 + o, 512))


def _cols_c(hh):
    r = lambda a, n: list(range(a, a + n))
    cols = []
    for rnd in range(2):
        o = hh * 1024 + rnd * 512
        cols += r(o, 512) + r(2048 + o, 512) + r(4096 + o, 512) + r(6144 + o, 512)
    return cols


_PERM_AB = list(range(0, 512)) + list(range(1024, 1536)) + list(range(512, 1024)) + list(range(1536, 2048))

_CACHE = {}


def _get(name, fn):
    if name not in _CACHE:
        _CACHE[name] = fn()
    return _CACHE[name]


def kernel_unfused(x, norm_ab, w_in_ab, w_out_ab, lam_ab, subln_ab, norm_c, w_in_c, w_out_c, final_norm):
    f32 = np.float32
    x = np.asarray(x, f32)
    tabs = _get("tabs", _tables)
    cores = list(range(8))
    tabin = dict(cos=tabs["cos"], sin=tabs["sin"], maskA=tabs["maskA"], maskC=tabs["maskC"], Emat=tabs["Emat"], ident=tabs["ident"])

    w0 = np.asarray(w_in_ab, f32)[0]
    wab = [np.ascontiguousarray(w0[:, _cols_ab(hh)]) for hh in range(2)]
    nc1 = _get("mix_ab", lambda: build_mixer("ab"))
    in1 = []
    for c in cores:
        b, hh = c // 2, c % 2
        d = dict(xin=x[b], g=np.asarray(norm_ab, f32)[0:1], w=wab[hh], lam=np.asarray(lam_ab, f32)[0].reshape(1, 512),
                 subln=np.asarray(subln_ab, f32)[0:1])
        d.update(tabin)
        in1.append(d)
    r1 = run_bass_kernel_spmd(nc1, in1, core_ids=cores).results
    yT0 = [np.concatenate([r1[2 * b]["yT"], r1[2 * b + 1]["yT"]], axis=0) for b in range(4)]

    wo0 = np.ascontiguousarray(np.asarray(w_out_ab, f32)[0][_PERM_AB, :])
    nc2 = _get("op0", lambda: build_outproj(False))
    in2 = []
    for c in cores:
        b, hh = c // 2, c % 2
        in2.append(dict(yTin=np.ascontiguousarray(yT0[b][:, hh * 1024:(hh + 1) * 1024]), res=x[b, hh * 1024:(hh + 1) * 1024], w=wo0))
    r2 = run_bass_kernel_spmd(nc2, in2, core_ids=cores).results
    h0 = [np.concatenate([r2[2 * b]["out"], r2[2 * b + 1]["out"]], axis=0) for b in range(4)]

    w1 = np.asarray(w_in_c, f32)[0]
    wc = [np.ascontiguousarray(w1[:, _cols_c(hh)]) for hh in range(2)]
    nc3 = _get("mix_c", lambda: build_mixer("c"))
    in3 = []
    for c in cores:
        b, hh = c // 2, c % 2
        d = dict(xin=h0[b], g=np.asarray(norm_c, f32)[0:1], w=wc[hh])
        d.update(tabin)
        in3.append(d)
    r3 = run_bass_kernel_spmd(nc3, in3, core_ids=cores).results
    yT1 = [np.concatenate([r3[2 * b]["yT"], r3[2 * b + 1]["yT"]], axis=0) for b in range(4)]

    wo1 = np.ascontiguousarray(np.asarray(w_out_c, f32)[0])
    nc4 = _get("op1", lambda: build_outproj(True))
    in4 = []
    for c in cores:
        b, hh = c // 2, c % 2
        in4.append(dict(yTin=np.ascontiguousarray(yT1[b][:, hh * 1024:(hh + 1) * 1024]), res=np.ascontiguousarray(h0[b][hh * 1024:(hh + 1) * 1024]),
                        w=wo1, g=np.asarray(final_norm, f32).reshape(1, D)))
    r4 = run_bass_kernel_spmd(nc4, in4, core_ids=cores).results
    out = np.stack([np.concatenate([r4[2 * b]["out"], r4[2 * b + 1]["out"]], axis=0) for b in range(4)], 0)
    return out.astype(f32)


def _fused_inputs(x, norm_ab, w_in_ab, w_out_ab, lam_ab, subln_ab, norm_c, w_in_c, w_out_c, final_norm):
    f32 = np.float32
    x = np.asarray(x, f32)
    tabs = _get("tabs", _tables)
    w0 = np.asarray(w_in_ab, f32)[0]
    w1 = np.asarray(w_in_c, f32)[0]
    wo0 = np.asarray(w_out_ab, f32)[0]
    wo1 = np.asarray(w_out_c, f32)[0][_PERM_AB, :]
    per_hh = []
    for hh in range(2):
        cs = slice(hh * 1024, (hh + 1) * 1024)
        per_hh.append(dict(
            w0=np.ascontiguousarray(w0[:, _cols_ab(hh)]), w1=np.ascontiguousarray(w1[:, _cols_c(hh)]),
            wo0=np.ascontiguousarray(wo0[:, cs]), wo1=np.ascontiguousarray(wo1[:, cs]),
            gc=np.ascontiguousarray(np.asarray(norm_c, f32)[0:1, cs]), gf=np.ascontiguousarray(np.asarray(final_norm, f32).reshape(1, D)[:, cs])))
    ins = []
    for c in range(8):
        b, hh = c // 2, c % 2
        d = dict(xin=x[b], xres=np.ascontiguousarray(x[b][:, hh * 1024:(hh + 1) * 1024]), g0=np.asarray(norm_ab, f32)[0:1],
                 lam=np.asarray(lam_ab, f32)[0].reshape(1, 512), subln=np.asarray(subln_ab, f32)[0:1],
                 cos=tabs["cos"], sin=tabs["sin"], maskA=tabs["maskA"], maskC=tabs["maskC"], Emat=tabs["Emat"], ident=tabs["ident"])
        d.update(per_hh[hh])
        ins.append(d)
    return ins


def kernel(x, norm_ab, w_in_ab, w_out_ab, lam_ab, subln_ab, norm_c, w_in_c, w_out_c, final_norm):
    ins = _fused_inputs(x, norm_ab, w_in_ab, w_out_ab, lam_ab, subln_ab, norm_c, w_in_c, w_out_c, final_norm)
    nc = _get("fused", build_fused)
    res = run_bass_kernel_spmd(nc, ins, core_ids=list(range(8))).results
    out = np.stack([np.concatenate([res[2 * b]["out"], res[2 * b + 1]["out"]], axis=1) for b in range(4)], 0)
    return out.astype(np.float32)
```

```python
import math
import os
DBG = set(os.environ.get('KDBG', '').split(','))
from contextlib import ExitStack

import numpy as np
import ml_dtypes

import concourse.bass as bass
import concourse.mybir as mybir
from concourse.bass_utils import run_bass_kernel_spmd

F32 = mybir.dt.float32
BF16 = mybir.dt.bfloat16
ALU = mybir.AluOpType
AF = mybir.ActivationFunctionType
AX = mybir.AxisListType
bf16_np = ml_dtypes.bfloat16

D = 2048
S_LEN = 2048
NT = 16
HD = 128
SCALE = HD ** -0.5
EPS = 1e-6
BIG = 32768.0
LAM_INIT0 = 0.8 - 0.6 * math.exp(-0.3 * 0)


class Res:
    __slots__ = ("w", "r", "name")

    def __init__(self, name=""):
        self.w = None
        self.r = []
        self.name = name


class DmaSlot:
    __slots__ = ("sem", "count", "key", "sw")

    def __init__(self, sem, key):
        self.sem = sem
        self.count = 0
        self.key = key


class Sched:
    ENG = ["pe", "act", "dve", "pool", "sp"]

    def __init__(self, nc, stack):
        self.nc = nc
        self.stack = stack
        self.sems = {}
        self.semobj = {}
        for e in self.ENG:
            s = stack.enter_context(nc.semaphore("s_" + e))
            self.sems[e] = "E" + e
            self.semobj["E" + e] = s
        self.cnt = {e: 0 for e in self.ENG}
        self.prog = {e: [] for e in self.ENG}
        self.seen = {e: {} for e in self.ENG}
        self.slots = []

    def recycle(self):
        self.free = {False: [sl for sl in self.slots if sl.sw is False], True: [sl for sl in self.slots if sl.sw is True]}

    def slot(self, sw=False):
        fr = getattr(self, "free", None)
        if fr and fr[sw]:
            return fr[sw].pop()
        key = "D%d" % (len(self.slots) + 1)
        s = self.stack.enter_context(self.nc.semaphore("d%d" % (len(self.slots) + 1)))
        self.semobj[key] = s
        sl = DmaSlot(s, key)
        sl.sw = sw
        self.slots.append(sl)
        return sl

    def _waits(self, eng, reads, writes):
        need = {}
        for r in reads:
            if r.w is not None:
                k, v = r.w
                need[k] = max(need.get(k, 0), v)
        for w in writes:
            if w.w is not None:
                k, v = w.w
                need[k] = max(need.get(k, 0), v)
            for (k, v) in w.r:
                need[k] = max(need.get(k, 0), v)
        out = []
        for k, v in need.items():
            if eng == "pe" and k == "Epe":
                continue
            if self.seen[eng].get(k, 0) >= v:
                continue
            self.seen[eng][k] = v
            out.append((k, v))
        return out

    def _record(self, tok, reads, writes):
        for r in reads:
            r.r.append(tok)
            if len(r.r) > 64:
                best = {}
                for (k, v) in r.r:
                    best[k] = max(best.get(k, 0), v)
                r.r = list(best.items())
        for w in writes:
            w.w = tok
            w.r = []

    def op(self, eng, fn, reads=(), writes=()):
        waits = self._waits(eng, reads, writes)
        self.cnt[eng] += 1
        tok = (self.sems[eng], self.cnt[eng])
        self._record(tok, reads, writes)
        self.prog[eng].append((waits, fn, (self.sems[eng], 1)))

    def dma(self, eng, fn, slot, reads=(), writes=()):
        waits = self._waits(eng, reads, writes)
        slot.count += 16
        tok = (slot.key, slot.count)
        self._record(tok, reads, writes)
        self.prog[eng].append((waits, fn, (slot.key, 16)))

    def cc(self, fn, reads=(), writes=()):
        sl = self.slot(sw=True) if False else None
        key = "C%d" % (len(self.slots) + 1)
        sem = self.stack.enter_context(self.nc.semaphore("c%d" % (len(self.slots) + 1)))
        self.semobj[key] = sem
        sl = DmaSlot(sem, key)
        sl.sw = None
        self.slots.append(sl)
        waits = self._waits("pool", reads, writes)
        sl.count += 1
        tok = (sl.key, sl.count)
        self._record(tok, reads, writes)
        self.prog["pool"].append((waits, fn, (sl.key, 1)))

    def barrier(self):
        tot = {self.sems[e]: self.cnt[e] for e in self.ENG}
        for sl in self.slots:
            tot[sl.key] = sl.count
        for e in self.ENG:
            waits = []
            for k, v in tot.items():
                if v == 0 or (k == self.sems[e]):
                    continue
                if self.seen[e].get(k, 0) >= v:
                    continue
                self.seen[e][k] = v
                waits.append((k, v))
            own = self.sems[e]
            if self.cnt[e] > 0 and self.seen[e].get(own, 0) < self.cnt[e]:
                self.seen[e][own] = self.cnt[e]
                waits.append((own, self.cnt[e]))
            self.prog[e].append((waits, None, None))

    def emit(self):
        nc = self.nc
        with nc.Block() as block:
            def mk(ename):
                def body(e):
                    for (waits, fn, inc) in self.prog[ename]:
                        for (k, v) in waits:
                            e.wait_ge(self.semobj[k], v)
                        if fn is None:
                            continue
                        ins = fn(e)
                        ins.then_inc(self.semobj[inc[0]], inc[1])
                return body
            block.tensor(mk("pe"))
            block.scalar(mk("act"))
            block.vector(mk("dve"))
            block.gpsimd(mk("pool"))
            block.sync(mk("sp"))


def _tables():
    inv = 1.0 / (10000.0 ** (np.arange(0, HD, 2, dtype=np.float32) / HD))
    ang = np.arange(S_LEN, dtype=np.float32)[:, None] * inv[None, :].astype(np.float32)
    cos = np.cos(ang).astype(np.float32)
    sin = np.sin(ang).astype(np.float32)
    kl = np.arange(128)[:, None]
    ql = np.arange(512)[None, :]

    def mult(d):
        c = (d <= 128).astype(np.float32) + ((d % 4 == 0) & (d <= 512)).astype(np.float32) \
            + (d % 16 == 0).astype(np.float32)
        return np.where(d >= 0, c, 0.0)

    mA = np.stack([mult(ql - kl + 128 * j) for j in range(-3, 6)], 0)
    mC = np.stack([(ql - kl + 128 * j >= 0).astype(np.float32) for j in (0, -1, -2, -3)], 0)
    E = np.zeros((128, 8, 128), np.float32)
    for n in range(8):
        E[n, n, :] = BIG
    return dict(
        cos=cos, sin=sin,
        maskA=np.ascontiguousarray(mA.transpose(1, 0, 2)).astype(bf16_np),
        maskC=np.ascontiguousarray(mC.transpose(1, 0, 2)).astype(bf16_np),
        Emat=E.astype(bf16_np),
        ident=np.eye(128, dtype=np.float32).astype(bf16_np),
    )


class Ctx:
    pass


def _mk_ctx(nc, st):
    K = Ctx()
    K.nc = nc
    K.st = st
    K.S = Sched(nc, st)
    K.banks = [st.enter_context(nc.psum_tensor("pb%d" % i, [128, 512], F32)) for i in range(6)]
    K.r_bank = [Res("bank%d" % i) for i in range(6)]
    K.ptrs = [st.enter_context(nc.psum_tensor("ptr%d" % i, [128, 8, 128], BF16)) for i in range(2)]
    K.r_ptr = [Res("ptrA"), Res("ptrB")]
    K.pm = K.banks[5]
    K.r_pm = K.r_bank[5]
    K.ntr = 0
    return K


def T_(K, st, name, shape, dt):
    K.uid = getattr(K, "uid", 0) + 1
    return st.enter_context(K.nc.sbuf_tensor("sb%d_%s" % (K.uid, name), shape, dt))


def load_const(K, st, name, ap, shape, dt, eng="act", late=None):
    t = T_(K, st, name, shape, dt)
    r = Res(name)
    issue = lambda: K.S.dma(eng if late is None else "sp", lambda e: e.dma_start(out=t[:], in_=ap), K.S.slot(), writes=[r])
    if late is None:
        issue()
    else:
        late.append(issue)
    return t, r


def transpose4(K, srcs, src_res, dst_ap_fn, dst_res, evac_eng, extra_w=()):
    S = K.S
    half = 0 if getattr(K, "single_ptr", False) else K.ntr % 2
    K.ntr += 1
    rp = K.r_ptr[half]
    n = len(srcs)
    for i, sap in enumerate(srcs):
        S.op("pe", lambda e, i=i, sap=sap, idn=K.ident: e.transpose(out=K.ptrs[half][:, i, :], in_=sap, identity=idn[:]),
             reads=list(src_res) + [K.r_ident], writes=[rp])
    pin = K.ptrs[half][:, 0:n, :]
    if evac_eng == "act":
        S.op("act", lambda e: e.copy(out=dst_ap_fn(), in_=pin), reads=[rp], writes=list(dst_res) + list(extra_w))
    else:
        S.op("dve", lambda e: e.tensor_copy(out=dst_ap_fn(), in_=pin), reads=[rp], writes=list(dst_res) + list(extra_w))


def mixer_phase(K, kind, xin, g_ap, w_ap, yT_out, tabs, extra, xnT_src=None, ag=None, pre_cc=None):
    nc, S = K.nc, K.S
    with ExitStack() as st:
        xnT = T_(K, st, "xnT", [128, 16, S_LEN], BF16)
        r_xnT = [Res("xnT%d" % t) for t in range(NT)]
        wsb = [T_(K, st, "wsb%d" % i, [128, 16, 512], BF16) for i in range(2)]
        r_w = [Res("w0"), Res("w1")]
        w_slot = [S.slot(True), S.slot(True)]
        late = []
        K.ident, K.r_ident = load_const(K, st, "ident", tabs["ident"], [128, 128], BF16)
        cos, r_cos = load_const(K, st, "cos", tabs["cos"].rearrange("(t p) d -> p t d", p=128), [128, 16, 64], F32, late=late)
        sin, r_sin = load_const(K, st, "sin", tabs["sin"].rearrange("(t p) d -> p t d", p=128), [128, 16, 64], F32, late=late)
        maskC, r_maskC = load_const(K, st, "maskC", tabs["maskC"], [128, 4, 512], BF16, late=late)
        if kind == "ab":
            maskA, r_maskA = load_const(K, st, "maskA", tabs["maskA"], [128, 9, 512], BF16, late=late)
            gsub, r_gsub = load_const(K, st, "gsub", extra["subln"].partition_broadcast(128), [128, 256], F32)
            lamv = T_(K, st, "lamv", [128, 8], F32)
            r_lamv = Res("lamv")
        else:
            Emat, r_E = load_const(K, st, "Emat", tabs["Emat"], [128, 8, 128], BF16, late=late)
        onesc = T_(K, st, "onesc", [128, 1], BF16)
        r_ones = Res("ones")
        S.op("dve" if "B" in DBG else "pool", lambda e: e.memset(onesc[:], 1.0), writes=[r_ones])

        def load_w(gidx, after=()):
            buf = gidx % 2
            c0 = gidx * 512
            src = w_ap[:, c0:c0 + 512].rearrange("(kc p) n -> p kc n", p=128)
            S.dma("pool", lambda e: e.dma_start(out=wsb[buf][:], in_=src), w_slot[buf], reads=list(after), writes=[r_w[buf]])

        if xnT_src is not None:
            for fn in late:
                fn()
            late = []
            load_w(0)
            if pre_cc:
                issue_ccs(K, pre_cc)
                S.barrier()

        if xnT_src is not None:
            for th in range(2):
                for i4 in range(4):
                    S.dma("sp", lambda e, i4=i4, th=th: e.dma_start(
                        out=xnT[:, i4 * 4:(i4 + 1) * 4, th * 1024:(th + 1) * 1024],
                        in_=xnT_src[th][i4 * 512:(i4 + 1) * 512, :].rearrange("(kc p) t -> p kc t", p=128)),
                        S.slot(), writes=r_xnT[th * 8:(th + 1) * 8])
        if xnT_src is not None:
            for fn in late:
                fn()
            late = []
        with ExitStack() as st1:
          if kind == "ab":
              lamt, r_lamt = load_const(K, st1, "lamt", extra["lam"].partition_broadcast(128), [128, 512], F32)
              lprod = T_(K, st1, "lprod", [128, 256], F32)
              r_lprod = Res("lprod")
              l4 = lamt[:].rearrange("p (a d) -> p a d", d=128)
              lp3 = lprod[:].rearrange("p (a d) -> p a d", d=128)
              S.op("dve", lambda e: e.tensor_tensor(out=lp3, in0=l4[:, 0::2, :], in1=l4[:, 1::2, :], op=ALU.mult),
                   reads=[r_lamt], writes=[r_lprod])
              S.op("dve", lambda e: e.reduce_sum(out=lamv[:, 0:2], in_=lp3, axis=AX.X), reads=[r_lprod], writes=[r_lamv])
              S.op("act", lambda e: e.activation(out=lamv[:, 2:4], in_=lamv[:, 0:2], func=AF.Exp), reads=[r_lamv], writes=[r_lamv])
              S.op("dve", lambda e: e.tensor_scalar(out=lamv[:, 4:5], in0=lamv[:, 3:4], scalar1=lamv[:, 2:3], scalar2=-LAM_INIT0,
                                                    op0=ALU.subtract, op1=ALU.add), reads=[r_lamv], writes=[r_lamv])
              S.op("dve", lambda e: e.tensor_scalar(out=gsub[:], in0=gsub[:], scalar1=1.0 - LAM_INIT0, scalar2=None, op0=ALU.mult),
                   reads=[r_gsub], writes=[r_gsub])

          if xnT_src is None:
              NXB = 4
              xs = [T_(K, st1, "xs%d" % i, [128, D], F32) for i in range(NXB)]
              r_xs = [Res("xs%d" % i) for i in range(NXB)]
              xs_slot = [S.slot() for _ in range(NXB)]
              gt, r_gt = load_const(K, st1, "gt", g_ap.partition_broadcast(128), [128, D], F32)
              sq = T_(K, st1, "sq", [128, D], BF16)
              r_sq = Res("sq")
              xnb = [T_(K, st1, "xnb%d" % i, [128, D], BF16) for i in range(2)]
              r_xnb = [Res("xnb0"), Res("xnb1")]
              stat = T_(K, st1, "stat", [128, 3 * NT], F32)
              r_stat = [Res("stat%d" % t) for t in range(NT)]
              for tt in range(NT):
                  b = tt % 2
                  xb = tt % NXB
                  S.dma("sp", lambda e, tt=tt, xb=xb: e.dma_start(out=xs[xb][:], in_=xin[tt * 128:(tt + 1) * 128, :]),
                        xs_slot[xb], writes=[r_xs[xb]])
                  if tt == 1:
                      load_w(0, after=[r_xs[1]])
                  if tt == 6:
                      for fn in late:
                          fn()
                      late = []
                  ss = stat[:, 3 * tt:3 * tt + 1]
                  ln = stat[:, 3 * tt + 1:3 * tt + 2]
                  rstd = stat[:, 3 * tt + 2:3 * tt + 3]
                  S.op("act", lambda e, xb=xb, ss=ss: e.activation(out=sq[:], in_=xs[xb][:], func=AF.Square, accum_out=ss),
                       reads=[r_xs[xb]], writes=[r_sq, r_stat[tt]])
                  if "A" in DBG:
                      S.op("dve", lambda e, rstd=rstd: e.memset(rstd, 1.0), writes=[r_stat[tt]])
                  else:
                      S.op("act", lambda e, ss=ss, ln=ln: e.activation(out=ln, in_=ss, func=AF.Ln, scale=1.0 / D, bias=K.epsb[:]),
                           reads=[r_stat[tt], K.r_epsb], writes=[r_stat[tt]])
                      S.op("act", lambda e, ln=ln, rstd=rstd: e.activation(out=rstd, in_=ln, func=AF.Exp, scale=-0.5),
                           reads=[r_stat[tt]], writes=[r_stat[tt]])
                  S.op("dve", lambda e, b=b, xb=xb, rstd=rstd: e.scalar_tensor_tensor(out=xnb[b][:], in0=xs[xb][:], scalar=rstd, in1=gt[:],
                                                                               op0=ALU.mult, op1=ALU.mult),
                       reads=[r_xs[xb], r_stat[tt], r_gt], writes=[r_xnb[b]])
                  for q4 in range(4):
                      srcs = [xnb[b][:, (q4 * 4 + i) * 128:(q4 * 4 + i + 1) * 128] for i in range(4)]
                      transpose4(K, srcs, [r_xnb[b]],
                                 lambda tt=tt, q4=q4: xnT[:, q4 * 4:q4 * 4 + 4, tt * 128:(tt + 1) * 128],
                                 [r_xnT[tt]], "act" if q4 % 2 else "dve")
        S.barrier()
        if getattr(K, 'stop', None) == 'norm':
            return

        with ExitStack() as st2:
            qT = T_(K, st2, "qT", [128, 4, S_LEN], BF16)
            kT = T_(K, st2, "kT", [128, 4, S_LEN], BF16)
            r_qT = [Res("qT%d" % t) for t in range(NT)]
            r_kT = [Res("kT%d" % t) for t in range(NT)]
            vbuf = T_(K, st2, "vbuf", [128, NT, 516], BF16)
            r_v = [Res("v%d" % t) for t in range(NT)]
            gz = T_(K, st2, "gz", [128, NT, 512], BF16)
            r_gz = [Res("gz%d" % t) for t in range(NT)]
            t1 = T_(K, st2, "t1", [128, 512], F32)
            t2 = T_(K, st2, "t2", [128, 512], F32)
            r_t1, r_t2 = Res("t1"), Res("t2")
            tok = [T_(K, st2, "tok%d" % i, [128, 512], BF16) for i in range(2)]
            r_tok = [Res("tok0"), Res("tok1")]
            pT = [T_(K, st2, "pT%d" % i, [128, 512], BF16) for i in range(3)]
            r_pT = [Res("pT%d" % i) for i in range(3)]
            ybf = [T_(K, st2, "ybf%d" % i, [128, 512], BF16) for i in range(2)]
            r_ybf = [Res("ybf0"), Res("ybf1")]
            yTs = T_(K, st2, "yTs", [128, 2, S_LEN], BF16)
            r_yTs = [Res("yTs0"), Res("yTs1")]
            y_slot = [S.slot(), S.slot()]
            sm = T_(K, st2, "sm", [128, 64], F32)
            r_sm = Res("sm")
            if kind == "ab":
                bu2 = T_(K, st2, "bu", [128, 2, 256], F32)
                byb2 = T_(K, st2, "byb", [128, 2, 256], F32)
                r_bu2, r_byb2 = [Res("bu0"), Res("bu1")], [Res("byb0"), Res("byb1")]
                btmp = T_(K, st2, "btmp", [128, 256], F32)
                r_btmp = Res("btmp")
            else:
                kms = T_(K, st2, "kms", [128, 64], F32)
                r_kms = Res("kms")
                kmT = T_(K, st2, "kmT", [128, 4, 8], BF16)
                r_kmT = Res("kmT")
                gm = T_(K, st2, "gm", [128, 4, 8, 8], F32)
                r_gm = Res("gm")
                m8 = T_(K, st2, "m8", [128, 4, 8, 8], F32)
                r_m8 = Res("m8")
                pen = T_(K, st2, "pen", [128, 4, NT, 8], BF16)
                r_pen = Res("pen")
                penT = [T_(K, st2, "penT%d" % i, [128, 512], BF16) for i in range(2)]
                r_penT = [Res("penT0"), Res("penT1")]
                for i in range(2):
                    S.op("pool", lambda e, i=i: e.memset(penT[i][:], 0.0), writes=[r_penT[i]])

            bank_rr = [0]
            tail = [None]
            r_ibp = [Res("ibp0"), Res("ibp1")]

            for rnd in range(2):
                rkind = "c" if kind == "c" else ("a" if rnd == 0 else "b")
                if rkind == "b":
                    vv = vbuf[:, :, 0:514].rearrange("p t (h d) -> p t h d", d=257)
                    nvh, dv = 2, 256
                else:
                    vv = vbuf[:, :, 0:516].rearrange("p t (h d) -> p t h d", d=129)
                    nvh, dv = 4, 128
                S.op("pool", lambda e, vv=vv, dv=dv: e.memset(vv[:, :, :, dv:dv + 1], 1.0), writes=list(r_v))

                for gi in range(4):
                    gidx = rnd * 4 + gi
                    if gidx + 1 < 8:
                        load_w(gidx + 1)
                    wb = wsb[gidx % 2]
                    rwb = r_w[gidx % 2]
                    for tt in range(NT):
                        bi = bank_rr[0] % 5
                        bank_rr[0] += 1
                        ps = K.banks[bi]
                        rps = K.r_bank[bi]
                        for kc in range(16):
                            S.op("pe", lambda e, ps=ps, kc=kc, tt=tt, wb=wb: e.matmul(
                                ps[:], lhsT=xnT[:, kc, tt * 128:(tt + 1) * 128], rhs=wb[:, kc, :], start=(kc == 0), stop=(kc == 15)),
                                reads=[r_xnT[tt], rwb], writes=[rps])
                        if tail[0] is not None:
                            tail[0]()
                            tail[0] = None
                        new_tail = None
                        if gi < 2:
                            ps4 = ps[:].rearrange("p (h t d) -> p h t d", h=4, t=2)
                            t14 = t1[:].rearrange("p (h t d) -> p h t d", h=4, t=2)
                            t24 = t2[:].rearrange("p (h t d) -> p h t d", h=4, t=2)
                            tb = tok[tt % 2]
                            rtb = r_tok[tt % 2]
                            tk4 = tb[:].rearrange("p (h t d) -> p h t d", h=4, t=2)
                            cosb = cos[:, tt, :].unsqueeze(1).unsqueeze(1).to_broadcast([128, 4, 2, 64])
                            sinb = sin[:, tt, :].unsqueeze(1).to_broadcast([128, 4, 64])
                            S.op("dve", lambda e, ps4=ps4, t14=t14, cosb=cosb: e.tensor_tensor(out=t14, in0=ps4, in1=cosb, op=ALU.mult),
                                 reads=[rps, r_cos], writes=[r_t1])
                            S.op("dve", lambda e, ps4=ps4, t24=t24, sinb=sinb: e.tensor_tensor(out=t24[:, :, 0, :], in0=ps4[:, :, 1, :], in1=sinb, op=ALU.mult),
                                 reads=[rps, r_sin], writes=[r_t2])
                            S.op("dve", lambda e, ps4=ps4, t24=t24, sinb=sinb: e.tensor_tensor(out=t24[:, :, 1, :], in0=ps4[:, :, 0, :], in1=sinb, op=ALU.mult),
                                 reads=[rps, r_sin], writes=[r_t2])
                            S.op("pool", lambda e, tk4=tk4, t14=t14, t24=t24: e.tensor_tensor(out=tk4[:, :, 0, :], in0=t14[:, :, 0, :], in1=t24[:, :, 0, :], op=ALU.subtract),
                                 reads=[r_t1, r_t2], writes=[rtb])
                            S.op("pool", lambda e, tk4=tk4, t14=t14, t24=t24: e.tensor_tensor(out=tk4[:, :, 1, :], in0=t14[:, :, 1, :], in1=t24[:, :, 1, :], op=ALU.add),
                                 reads=[r_t1, r_t2], writes=[rtb])
                            dstT = qT if gi == 0 else kT
                            rdst = (r_qT if gi == 0 else r_kT)[tt]

                            def pe_tail(dstT=dstT, rdst=rdst, tb=tb, rtb=rtb, tt=tt, gi=gi, rkind=rkind):
                                transpose4(K, [tb[:, h * 128:(h + 1) * 128] for h in range(4)], [rtb],
                                           lambda: dstT[:, 0:4, tt * 128:(tt + 1) * 128], [rdst], "act")
                                if gi == 1 and rkind == "c":
                                    for h in range(4):
                                        S.op("pe", lambda e, h=h: e.matmul(
                                            K.pm[:, h * 16 + tt:h * 16 + tt + 1], lhsT=tb[:, h * 128:(h + 1) * 128], rhs=onesc[:], start=True, stop=True),
                                            reads=[rtb, r_ones], writes=[K.r_pm])
                            new_tail = pe_tail
                        elif gi == 2:
                            psv = ps[:].rearrange("p (h d) -> p h d", d=dv)
                            S.op("act", lambda e, psv=psv, vv=vv, tt=tt, dv=dv: e.copy(out=vv[:, tt, :, 0:dv], in_=psv),
                                 reads=[rps], writes=[r_v[tt]])
                        else:
                            S.op("act", lambda e, ps=ps, tt=tt: e.activation(out=gz[:, tt, :], in_=ps[:], func=AF.Silu),
                                 reads=[rps], writes=[r_gz[tt]])
                        tail[0] = new_tail
                if tail[0] is not None:
                    tail[0]()
                    tail[0] = None

                if getattr(K, 'stop', None) == 'proj%d' % rnd:
                    S.barrier()
                    return
                if rkind == "c":
                    S.op("act", lambda e: e.copy(out=kms[:], in_=K.pm[:, 0:64]), reads=[K.r_pm], writes=[r_kms])
                    k4 = kms[:].rearrange("p (h b t) -> p h b t", h=4, t=2)
                    S.op("dve", lambda e: e.tensor_tensor(out=gm[:, :, 0, :], in0=k4[:, :, :, 0], in1=k4[:, :, :, 1], op=ALU.add),
                         reads=[r_kms], writes=[r_gm])
                    S.op("dve", lambda e: e.tensor_scalar(out=kmT[:], in0=gm[:, :, 0, :], scalar1=1.0 / 256.0, scalar2=None, op0=ALU.mult),
                         reads=[r_gm], writes=[r_kmT])
                    for h in range(4):
                        for t8 in range(8):
                            tt = 8 + t8
                            S.op("pe", lambda e, h=h, t8=t8, tt=tt: e.matmul(
                                K.pm[:, 64 + (h * 8 + t8) * 8:64 + (h * 8 + t8) * 8 + 8], lhsT=qT[:, h, tt * 128:(tt + 1) * 128], rhs=kmT[:, h, :],
                                start=True, stop=True), reads=[r_qT[tt], r_kmT], writes=[K.r_pm])
                    S.op("pool", lambda e: e.memset(gm[:], -1.0e30), reads=[r_kmT], writes=[r_gm])
                    S.op("pool", lambda e: e.memset(pen[:], -1.0), writes=[r_pen])
                    pg = K.pm[:, 64:320].rearrange("p (h t n) -> p h t n", h=4, t=8)
                    for t8 in range(8):
                        qb = (8 + t8) // 2
                        S.op("dve", lambda e, t8=t8, qb=qb: e.tensor_copy(out=gm[:, :, t8, 0:qb], in_=pg[:, :, t8, 0:qb]),
                             reads=[K.r_pm], writes=[r_gm])
                    for h in range(4):
                        for t8 in range(8):
                            S.op("dve", lambda e, h=h, t8=t8: e.max(out=m8[:, h, t8, :], in_=gm[:, h, t8, :]), reads=[r_gm], writes=[r_m8])
                    for h in range(4):
                        for t8 in range(8):
                            S.op("dve", lambda e, h=h, t8=t8: e.tensor_scalar(
                                out=pen[:, h, 8 + t8, :], in0=gm[:, h, t8, :], scalar1=m8[:, h, t8, 2:3], scalar2=1.0,
                                op0=ALU.is_ge, op1=ALU.subtract), reads=[r_gm, r_m8], writes=[r_pen])
                    for tt in range(NT):
                        qb = tt // 2
                        hi = qb + 1 if tt < 8 else None
                        if tt < 8:
                            S.op("pool", lambda e, tt=tt, qb=qb: e.memset(pen[:, :, tt, 0:qb + 1], 0.0), writes=[r_pen])
                        else:
                            S.op("pool", lambda e, tt=tt, qb=qb: e.memset(pen[:, :, tt, qb:qb + 1], 0.0), writes=[r_pen])

                QC = 256 if rkind == "b" else 512
                nqt = QC // 128
                nchunk = S_LEN // QC
                blocks = []
                if rkind == "b":
                    units = [(hb, [(2 * hb + p, hb, p) for p in range(2)]) for hb in range(2)]
                else:
                    units = [(h, [(h, h, 0)]) for h in range(4)]
                for (uh, passes) in units:
                    for c in range(nchunk):
                        nkt = (c + 1) * nqt
                        cblocks = []
                        for (qi, vh, p) in passes:
                            for kt in range(nkt):
                                cblocks.append(dict(uh=uh, c=c, qi=qi, vh=vh, p=p, kt=kt, last=False, first_of_chunk=False))
                        cblocks[0]["first_of_chunk"] = True
                        cblocks[-1]["last"] = True
                        blocks += cblocks

                pend = []
                chunk_ctr = [0]

                def acc_ap(bl, qt):
                    if rkind == "b":
                        return K.banks[2 + 2 * bl["p"] + qt][:, 0:257], K.r_bank[2 + 2 * bl["p"] + qt]
                    par = bl["cpar"]
                    bk = 2 + 2 * par + qt // 2
                    return K.banks[bk][:, 0:258].rearrange("p (a d) -> p a d", d=129)[:, qt % 2, :], K.r_bank[bk]

                def emit_pen_prep(uh, c, slot):
                    srcs = [pen[:, uh, c * 4 + i, :] for i in range(4)]
                    half = 0
                    K.ntr += 1
                    rp = K.r_ptr[half]
                    for i, sap in enumerate(srcs):
                        S.op("pe", lambda e, i=i, sap=sap, half=half, idn=K.ident: e.transpose(out=K.ptrs[half][0:8, i, :], in_=sap, identity=idn[:]),
                             reads=[r_pen, K.r_ident], writes=[rp])
                    S.op("dve", lambda e, half=half, slot=slot: e.tensor_copy(
                        out=penT[slot][0:8, :].rearrange("p (a d) -> p a d", d=128), in_=K.ptrs[half][0:8, 0:4, :]),
                        reads=[rp], writes=[r_penT[slot]])

                sT3 = K.ptrs[1][:].rearrange("p a b -> p (a b)").bitcast(F32)
                sbanks = [(K.banks[0][:], K.r_bank[0]), (K.banks[1][:], K.r_bank[1]), (sT3, K.r_ptr[1])]

                def emit_qk(i):
                    bl = blocks[i]
                    ps, rps = sbanks[i % 3]
                    c, kt, qi = bl["c"], bl["kt"], bl["qi"]
                    if bl["first_of_chunk"]:
                        bl["cidx"] = chunk_ctr[0]
                        chunk_ctr[0] += 1
                        if rkind == "c" and c >= 2:
                            emit_pen_prep(bl["uh"], c, bl["cidx"] % 2)
                    else:
                        bl["cidx"] = blocks[i - 1]["cidx"]
                    bl["cpar"] = bl["cidx"] % 2
                    use_pen = (rkind == "c") and c >= 2 and (kt // 2 < (c * 4 + 3) // 2)
                    S.op("pe", lambda e, ps=ps, qi=qi, kt=kt, c=c, use_pen=use_pen, QC=QC: e.matmul(
                        ps[:, 0:QC], lhsT=kT[:, qi, kt * 128:(kt + 1) * 128], rhs=qT[:, qi, c * QC:(c + 1) * QC],
                        start=True, stop=(not use_pen)),
                        reads=[r_kT[kt]] + [r_qT[c * nqt + j] for j in range(nqt)], writes=[rps])
                    if use_pen:
                        slot = bl["cpar"]
                        S.op("pe", lambda e, ps=ps, kt=kt, slot=slot, QC=QC: e.matmul(
                            ps[:, 0:QC], lhsT=Emat[:, kt // 2, :], rhs=penT[slot][:], start=False, stop=True),
                            reads=[r_E, r_penT[slot]], writes=[rps])

                def emit_rest(i):
                    bl = blocks[i]
                    ps, rps = sbanks[i % 3]
                    pb = i % 3
                    c, kt, vh = bl["c"], bl["kt"], bl["vh"]
                    S.op("act", lambda e, ps=ps, pb=pb, QC=QC: e.activation(out=pT[pb][:, 0:QC], in_=ps[:, 0:QC], func=AF.Exp, scale=SCALE),
                         reads=[rps], writes=[r_pT[pb]])
                    j = c * nqt - kt
                    if rkind == "a":
                        mk = maskA[:, min(j, 5) + 3, :]
                        rmk = r_maskA
                    elif j <= 0:
                        mk = maskC[:, -j, 0:QC]
                        rmk = r_maskC
                    else:
                        mk = None
                    if mk is not None:
                        S.op("dve", lambda e, pb=pb, mk=mk, QC=QC: e.tensor_tensor(out=pT[pb][:, 0:QC], in0=pT[pb][:, 0:QC], in1=mk, op=ALU.mult),
                             reads=[rmk], writes=[r_pT[pb]])
                    for qt in range(nqt):
                        if kt > c * nqt + qt:
                            continue
                        acc, racc = acc_ap(bl, qt)
                        if rkind == "b":
                            st_flag = (kt == 0)
                        else:
                            st_flag = (kt == 0 and qt % 2 == 0)
                        sp_flag = (kt == c * nqt + qt)
                        S.op("pe", lambda e, acc=acc, pb=pb, qt=qt, kt=kt, vh=vh, st_flag=st_flag, sp_flag=sp_flag, vv=vv: e.matmul(
                            acc, lhsT=pT[pb][:, qt * 128:(qt + 1) * 128], rhs=vv[:, kt, vh, :], start=st_flag, stop=sp_flag, skip_group_check=True),
                            reads=[r_pT[pb], r_v[kt]], writes=[racc])
                    if bl["last"]:
                        finalize(i, bl)

                def finalize(i, bl):
                    c, uh = bl["c"], bl["uh"]
                    yb_i = bl["cidx"] % 2
                    yb_t = ybf[yb_i]
                    r_yb = r_ybf[yb_i]
                    if rkind != "b":
                        for qt in range(4):
                            tt = c * 4 + qt
                            acc, racc = acc_ap(bl, qt)
                            rc = sm[:, qt:qt + 1]
                            S.op("dve", lambda e, acc=acc, rc=rc: e.reciprocal(out=rc, in_=acc[:, 128:129]), reads=[racc], writes=[r_sm])
                            S.op("dve", lambda e, acc=acc, rc=rc, qt=qt, tt=tt, uh=uh, yb_t=yb_t: e.scalar_tensor_tensor(
                                out=yb_t[:, qt * 128:(qt + 1) * 128], in0=acc[:, 0:128], scalar=rc, in1=gz[:, tt, uh * 128:(uh + 1) * 128],
                                op0=ALU.mult, op1=ALU.mult), reads=[racc, r_sm, r_gz[tt]], writes=[r_yb])
                        ysl = uh % 2

                        def tr(c=c, uh=uh, yb_t=yb_t, r_yb=r_yb, ysl=ysl):
                            transpose4(K, [yb_t[:, q * 128:(q + 1) * 128] for q in range(4)], [r_yb],
                                       lambda: yTs[:, ysl, c * 512:(c + 1) * 512].rearrange("p (a d) -> p a d", d=128),
                                       [r_yTs[ysl]], "dve")
                            if c == nchunk - 1:
                                row0 = rnd * 512 + uh * 128
                                S.dma("sp", lambda e: e.dma_start(out=yT_out.get(row0, 128), in_=yTs[:, ysl, :]),
                                      y_slot[ysl], reads=[r_yTs[ysl]], writes=[r_ibp[row0 // 512]])
                        pend.append((i + 2, tr))
                    else:
                        for ph in range(3):
                          for qt in range(2):
                            tt = c * 2 + qt
                            a1, ra1 = K.banks[2 + qt][:, 0:257], K.r_bank[2 + qt]
                            a2, ra2 = K.banks[4 + qt][:, 0:257], K.r_bank[4 + qt]
                            bu, byb, r_bu, r_byb = bu2[:, qt, :], byb2[:, qt, :], r_bu2[qt], r_byb2[qt]
                            o = 8 * qt
                            r1 = sm[:, o:o + 1]
                            r2 = sm[:, o + 1:o + 2]
                            ssq = sm[:, o + 2:o + 3]
                            lnv = sm[:, o + 3:o + 4]
                            rstd = sm[:, o + 4:o + 5]
                            if ph == 0:
                                S.op("dve", lambda e, a1=a1, r1=r1: e.reciprocal(out=r1, in_=a1[:, 256:257]), reads=[ra1], writes=[r_sm])
                                S.op("dve", lambda e, a1=a1, r1=r1, bu=bu: e.tensor_scalar(out=bu, in0=a1[:, 0:256], scalar1=r1, scalar2=None, op0=ALU.mult),
                                     reads=[ra1, r_sm], writes=[r_bu])
                                continue
                            if ph == 1:
                                S.op("dve", lambda e, a2=a2, r2=r2: e.reciprocal(out=r2, in_=a2[:, 256:257]), reads=[ra2], writes=[r_sm])
                                S.op("dve", lambda e, r2=r2: e.tensor_tensor(out=r2, in0=r2, in1=lamv[:, 4:5], op=ALU.mult), reads=[r_lamv], writes=[r_sm])
                                S.op("dve", lambda e, a2=a2, r2=r2, bu=bu, byb=byb: e.scalar_tensor_tensor(out=byb, in0=a2[:, 0:256], scalar=r2, in1=bu,
                                                                                                  op0=ALU.mult, op1=ALU.add),
                                     reads=[ra2, r_sm, r_bu], writes=[r_byb])
                                continue
                            S.op("dve", lambda e, ssq=ssq, byb=byb, bu=bu: e.scalar_tensor_tensor(out=bu, in0=byb, scalar=1.0, in1=byb,
                                                                                  op0=ALU.mult, op1=ALU.mult, accum_out=ssq),
                                 reads=[r_byb], writes=[r_bu, r_sm])

                        def stage2(c=c, uh=uh, yb_t=yb_t, r_yb=r_yb):
                            for qt in range(2):
                                tt = c * 2 + qt
                                byb, r_byb = byb2[:, qt, :], r_byb2[qt]
                                o = 8 * qt
                                ssq = sm[:, o + 2:o + 3]
                                lnv = sm[:, o + 3:o + 4]
                                rstd = sm[:, o + 4:o + 5]
                                S.op("act", lambda e, ssq=ssq, lnv=lnv: e.activation(out=lnv, in_=ssq, func=AF.Ln, scale=1.0 / 256.0, bias=K.epsb[:]),
                                     reads=[r_sm, K.r_epsb], writes=[r_sm])
                                S.op("act", lambda e, lnv=lnv, rstd=rstd: e.activation(out=rstd, in_=lnv, func=AF.Exp, scale=-0.5), reads=[r_sm], writes=[r_sm])
                                S.op("pool", lambda e, rstd=rstd, byb=byb: e.tensor_scalar(out=btmp[:], in0=byb, scalar1=rstd, scalar2=1.0,
                                                                                       op0=ALU.mult, op1=ALU.mult),
                                     reads=[r_byb, r_sm], writes=[r_btmp])
                                S.op("pool", lambda e: e.tensor_tensor(out=btmp[:], in0=btmp[:], in1=gsub[:], op=ALU.mult),
                                     reads=[r_gsub], writes=[r_btmp])
                                S.op("pool", lambda e, qt=qt, tt=tt: e.tensor_tensor(
                                    out=yb_t[:, qt * 256:(qt + 1) * 256], in0=btmp[:], in1=gz[:, tt, uh * 256:(uh + 1) * 256], op=ALU.mult),
                                    reads=[r_btmp, r_gz[tt]], writes=[r_yb])
                            pend.append((stage2.due3, trb))

                        def trb(c=c, uh=uh, yb_t=yb_t, r_yb=r_yb):
                            half = 0
                            K.ntr += 1
                            rp = K.r_ptr[half]
                            for i4 in range(4):
                                S.op("pe", lambda e, i4=i4, half=half, idn=K.ident: e.transpose(out=K.ptrs[half][:, i4, :], in_=yb_t[:, i4 * 128:(i4 + 1) * 128],
                                                                                   identity=idn[:]), reads=[r_yb, K.r_ident], writes=[rp])
                            for fh in range(2):
                                S.op("dve", lambda e, fh=fh, half=half: e.tensor_copy(
                                    out=yTs[:, fh, c * 256:(c + 1) * 256].rearrange("p (a d) -> p a d", d=128),
                                    in_=K.ptrs[half][:, fh:4:2, :]), reads=[rp], writes=[r_yTs[fh]])
                            if c == nchunk - 1:
                                for fh in range(2):
                                    row0 = rnd * 512 + uh * 256 + fh * 128
                                    S.dma("sp", lambda e, fh=fh, row0=row0: e.dma_start(out=yT_out.get(row0, 128), in_=yTs[:, fh, :]),
                                          y_slot[fh], reads=[r_yTs[fh]], writes=[r_ibp[row0 // 512]])
                        nxt = 0
                        j2 = i + 1
                        while j2 < len(blocks):
                            nxt += 1
                            if blocks[j2]["last"]:
                                break
                            j2 += 1
                        k1 = max(2, min(8, nxt - 1))
                        stage2.due3 = i + k1 + 3
                        pend.append((i + k1, stage2))

                nb = len(blocks)
                K.single_ptr = True
                emit_qk(0)
                emit_qk(1)
                for i in range(nb):
                    if i + 2 < nb:
                        emit_qk(i + 2)
                    due = [p for p in pend if p[0] <= i]
                    pend[:] = [p for p in pend if p[0] > i]
                    for (_, fn) in due:
                        fn()
                    emit_rest(i)
                for (_, fn) in pend:
                    fn()
                pend[:] = []
                K.single_ptr = False
                if ag is not None and rnd == 0:
                    src0, dst0 = ag[0].pieces[0], ag[1].pieces[0]
                    S.cc(lambda e: e.collective_compute("AllGather", ALU.bypass, replica_groups=RG_PAIRS, ins=[src0], outs=[dst0]),
                         reads=[r_ibp[0]])
        S.barrier()
        S.recycle()


def outproj_phase(K, final, yT_in, res_in, w_ap, out_ap, tabs, g_ap=None):
    nc, S = K.nc, K.S
    with ExitStack() as st:
        yTi = T_(K, st, "yTi", [128, 16, 1024], BF16)
        r_yTi = [Res("yTi%d" % i) for i in range(4)]
        for i in range(4):
            S.dma("sp", lambda e, i=i: e.dma_start(out=yTi[:, i * 4:(i + 1) * 4, :],
                                                   in_=yT_in[i * 512:(i + 1) * 512, :].rearrange("(kc p) t -> p kc t", p=128)),
                  S.slot(), writes=[r_yTi[i]])
        wo = T_(K, st, "wo", [128, 4, 16, 512], BF16)
        r_wo = [Res("wo%d" % i) for i in range(4)]
        for cg in range(4):
            S.dma("pool", lambda e, cg=cg: e.dma_start(out=wo[:, cg, :, :],
                                                       in_=w_ap[:, cg * 512:(cg + 1) * 512].rearrange("(kc p) n -> p kc n", p=128)),
                  S.slot(True), writes=[r_wo[cg]])
        res = T_(K, st, "res", [128, 8, D], F32)
        r_res = [Res("res%d" % i) for i in range(8)]
        for tt in range(8):
            S.dma("sp", lambda e, tt=tt: e.dma_start(out=res[:, tt, :], in_=res_in[tt * 128:(tt + 1) * 128, :]), S.slot(), writes=[r_res[tt]])
        o_slot = [S.slot(), S.slot()]
        if final:
            gt, r_gt = load_const(K, st, "gfin", g_ap.partition_broadcast(128), [128, D], F32)
            sq = T_(K, st, "sq2", [128, D], BF16)
            r_sq = Res("sq2")
            ob = [T_(K, st, "ob%d" % i, [128, D], F32) for i in range(2)]
            r_ob = [Res("ob0"), Res("ob1")]
            stat = T_(K, st, "stat2", [128, 24], F32)
            r_stat = Res("stat2")
        cnt = 0
        for tt in range(8):
            for cg in range(4):
                bi = cnt % 6
                cnt += 1
                ps, rps = K.banks[bi], K.r_bank[bi]
                for kc in range(16):
                    S.op("pe", lambda e, ps=ps, kc=kc, tt=tt, cg=cg: e.matmul(
                        ps[:], lhsT=yTi[:, kc, tt * 128:(tt + 1) * 128], rhs=wo[:, cg, kc, :], start=(kc == 0), stop=(kc == 15)),
                        reads=[r_yTi[kc // 4], r_wo[cg]], writes=[rps])
                S.op("dve", lambda e, ps=ps, tt=tt, cg=cg: e.tensor_tensor(out=res[:, tt, cg * 512:(cg + 1) * 512], in0=ps[:],
                                                                           in1=res[:, tt, cg * 512:(cg + 1) * 512], op=ALU.add),
                     reads=[rps], writes=[r_res[tt]])
            if not final:
                S.dma("sp", lambda e, tt=tt: e.dma_start(out=out_ap[tt * 128:(tt + 1) * 128, :], in_=res[:, tt, :]), o_slot[tt % 2], reads=[r_res[tt]])
            else:
                ss = stat[:, 3 * tt:3 * tt + 1]
                ln = stat[:, 3 * tt + 1:3 * tt + 2]
                rstd = stat[:, 3 * tt + 2:3 * tt + 3]
                b = tt % 2
                S.op("act", lambda e, tt=tt, ss=ss: e.activation(out=sq[:], in_=res[:, tt, :], func=AF.Square, accum_out=ss),
                     reads=[r_res[tt]], writes=[r_sq, r_stat])
                S.op("act", lambda e, ss=ss, ln=ln: e.activation(out=ln, in_=ss, func=AF.Ln, scale=1.0 / D, bias=K.epsb[:]),
                     reads=[r_stat, K.r_epsb], writes=[r_stat])
                S.op("act", lambda e, ln=ln, rstd=rstd: e.activation(out=rstd, in_=ln, func=AF.Exp, scale=-0.5), reads=[r_stat], writes=[r_stat])
                S.op("dve", lambda e, tt=tt, b=b, rstd=rstd: e.scalar_tensor_tensor(out=ob[b][:], in0=res[:, tt, :], scalar=rstd, in1=gt[:],
                                                                                   op0=ALU.mult, op1=ALU.mult),
                     reads=[r_res[tt], r_stat, r_gt], writes=[r_ob[b]])
                S.dma("sp", lambda e, tt=tt, b=b: e.dma_start(out=out_ap[tt * 128:(tt + 1) * 128, :], in_=ob[b][:]), o_slot[b], reads=[r_ob[b]])
    S.barrier()


RG_PAIRS = [[0, 1], [2, 3], [4, 5], [6, 7]]


class Rows:
    def __init__(self, pieces, P):
        self.pieces = pieces
        self.P = P

    def get(self, a, n):
        assert a // self.P == (a + n - 1) // self.P
        return self.pieces[a // self.P][a % self.P:a % self.P + n, :]


def issue_ccs(K, pairs):
    for (sp_, dp_) in pairs:
        K.S.cc(lambda e, sp_=sp_, dp_=dp_: e.collective_compute("AllGather", ALU.bypass, replica_groups=RG_PAIRS, ins=[sp_], outs=[dp_]))


def collective_ag_rows(K, src, dst, pieces=(0, 1)):
    S = K.S
    S.barrier()
    for sp_, dp_ in [(src.pieces[i], dst.pieces[i]) for i in pieces]:
        S.op("pool", lambda e, sp_=sp_, dp_=dp_: e.collective_compute("AllGather", ALU.bypass, replica_groups=RG_PAIRS, ins=[sp_], outs=[dp_]))
    S.barrier()


def collective_ag(K, src_ap, dst_ap):
    S = K.S
    S.barrier()
    S.op("pool", lambda e: e.collective_compute("AllGather", ALU.bypass, replica_groups=RG_PAIRS, ins=[src_ap], outs=[dst_ap]))
    S.barrier()


def outproj_cols(K, G, w_ap, res_ap, g_ap, final, out_ap, hd_ap, ss_ib, ss_ob, pre_cc=None):
    nc, S = K.nc, K.S
    with ExitStack() as st:
        hown = T_(K, st, "hown", [128, NT, 1024], F32)
        r_h = [Res("h%d" % t) for t in range(NT)]
        hsl = [S.slot() for _ in range(NT)]
        for tt in range(NT):
            S.dma("sp", lambda e, tt=tt: e.dma_start(out=hown[:, tt, :], in_=res_ap[tt * 128:(tt + 1) * 128, :]), hsl[tt], writes=[r_h[tt]])
        stat = T_(K, st, "ostat", [128, 64], F32)
        r_stat = Res("ostat")
        gt, r_gt = load_const(K, st, "og", g_ap.partition_broadcast(128), [128, 1024], F32)
        K.ident, K.r_ident = load_const(K, st, "oident", K.tabs["ident"], [128, 128], BF16)
        with ExitStack() as st1:
            yTi = T_(K, st1, "yTi", [128, 16, S_LEN], BF16)
            r_yTi = [Res("yTi%d" % i) for i in range(4)]
            wo = T_(K, st1, "wo", [128, 2, 16, 512], BF16)
            r_wo = [Res("wo0"), Res("wo1")]
            if pre_cc:
                issue_ccs(K, pre_cc)
            for cg in range(2):
                S.dma("pool", lambda e, cg=cg: e.dma_start(out=wo[:, cg, :, :],
                                                           in_=w_ap[:, cg * 512:(cg + 1) * 512].rearrange("(kc p) n -> p kc n", p=128)),
                      S.slot(True), writes=[r_wo[cg]])
            if pre_cc:
                S.barrier()
            for i in range(4):
                S.dma("sp", lambda e, i=i: e.dma_start(out=yTi[:, i * 4:(i + 1) * 4, :],
                                                       in_=G.get(i * 512, 512).rearrange("(kc p) t -> p kc t", p=128)),
                      S.slot(), writes=[r_yTi[i]])
            sq = T_(K, st1, "osq", [128, 1024], BF16)
            r_sq = Res("osq")
            cnt = 0
            for tt in range(NT):
                for cg in range(2):
                    bi = cnt % 6
                    cnt += 1
                    ps, rps = K.banks[bi], K.r_bank[bi]
                    for kc in range(16):
                        S.op("pe", lambda e, ps=ps, kc=kc, tt=tt, cg=cg: e.matmul(
                            ps[:], lhsT=yTi[:, kc, tt * 128:(tt + 1) * 128], rhs=wo[:, cg, kc, :], start=(kc == 0), stop=(kc == 15)),
                            reads=[r_yTi[kc // 4], r_wo[cg]], writes=[rps])
                    S.op("dve", lambda e, ps=ps, tt=tt, cg=cg: e.tensor_tensor(out=hown[:, tt, cg * 512:(cg + 1) * 512], in0=ps[:],
                                                                               in1=hown[:, tt, cg * 512:(cg + 1) * 512], op=ALU.add),
                         reads=[rps], writes=[r_h[tt]])
                S.op("act", lambda e, tt=tt: e.activation(out=sq[:], in_=hown[:, tt, :], func=AF.Square, accum_out=stat[:, tt:tt + 1]),
                     reads=[r_h[tt]], writes=[r_sq, r_stat])
                if not final:
                    S.dma("sp", lambda e, tt=tt: e.dma_start(out=hd_ap[tt * 128:(tt + 1) * 128, :], in_=hown[:, tt, :]), hsl[tt], reads=[r_h[tt]])
        S.dma("sp", lambda e: e.dma_start(out=ss_ib, in_=stat[:, 0:16]), S.slot(), reads=[r_stat])
        collective_ag(K, ss_ib, ss_ob)
        S.dma("sp", lambda e: e.dma_start(out=stat[:, 16:48].rearrange("p (r t) -> p r t", r=2), in_=ss_ob.rearrange("(r p) t -> p r t", p=128)),
              S.slot(), writes=[r_stat])
        S.op("dve", lambda e: e.tensor_tensor(out=stat[:, 48:64], in0=stat[:, 16:32], in1=stat[:, 32:48], op=ALU.add), reads=[r_stat], writes=[r_stat])
        S.op("act", lambda e: e.activation(out=stat[:, 48:64], in_=stat[:, 48:64], func=AF.Ln, scale=1.0 / D, bias=K.epsb[:]),
             reads=[r_stat, K.r_epsb], writes=[r_stat])
        S.op("act", lambda e: e.activation(out=stat[:, 48:64], in_=stat[:, 48:64], func=AF.Exp, scale=-0.5), reads=[r_stat], writes=[r_stat])
        with ExitStack() as st2:
            if final:
                ob = [T_(K, st2, "ob%d" % i, [128, 1024], F32) for i in range(2)]
                r_ob = [Res("ob0"), Res("ob1")]
                o_slot = [S.slot(), S.slot()]
                for tt in range(NT):
                    b = tt % 2
                    S.op("dve", lambda e, tt=tt, b=b: e.scalar_tensor_tensor(out=ob[b][:], in0=hown[:, tt, :], scalar=stat[:, 48 + tt:49 + tt], in1=gt[:],
                                                                            op0=ALU.mult, op1=ALU.mult),
                         reads=[r_h[tt], r_stat, r_gt], writes=[r_ob[b]])
                    S.dma("sp", lambda e, tt=tt, b=b: e.dma_start(out=out_ap[tt * 128:(tt + 1) * 128, :], in_=ob[b][:]), o_slot[b], reads=[r_ob[b]])
            else:
                xnb = [T_(K, st2, "oxnb%d" % i, [128, 1024], BF16) for i in range(2)]
                r_xnb = [Res("oxnb0"), Res("oxnb1")]
                xnTo = T_(K, st2, "xnTo", [128, 8, S_LEN], BF16)
                r_xo = [Res("xnTo0"), Res("xnTo1")]
                for tt in range(NT):
                    b = tt % 2
                    S.op("dve", lambda e, tt=tt, b=b: e.scalar_tensor_tensor(out=xnb[b][:], in0=hown[:, tt, :], scalar=stat[:, 48 + tt:49 + tt], in1=gt[:],
                                                                            op0=ALU.mult, op1=ALU.mult),
                         reads=[r_h[tt], r_stat, r_gt], writes=[r_xnb[b]])
                    for q4 in range(2):
                        srcs = [xnb[b][:, (q4 * 4 + i) * 128:(q4 * 4 + i + 1) * 128] for i in range(4)]
                        transpose4(K, srcs, [r_xnb[b]], lambda tt=tt, q4=q4: xnTo[:, q4 * 4:q4 * 4 + 4, tt * 128:(tt + 1) * 128],
                                   [r_xo[tt // 8]], "act" if q4 % 2 else "dve")
                    if tt % 8 == 7:
                        th = tt // 8
                        r_piece = Res("ib2p%d" % th)
                        S.dma("sp", lambda e, th=th: e.dma_start(out=out_ap[0][th].rearrange("(kc p) t -> p kc t", p=128),
                                                                 in_=xnTo[:, :, th * 1024:(th + 1) * 1024]), S.slot(), reads=[r_xo[th]], writes=[r_piece])
                        if th == 0:
                            src0, dst0 = out_ap[0][0], out_ap[1][0]
                            S.cc(lambda e: e.collective_compute("AllGather", ALU.bypass, replica_groups=RG_PAIRS, ins=[src0], outs=[dst0]),
                                 reads=[r_piece])
    S.barrier()
    S.recycle()


def build_fused():
    nc = bass.Bass("TRN2", target_bir_lowering=False)
    xin = _dram_in(nc, "xin", [S_LEN, D], F32)
    xres = _dram_in(nc, "xres", [S_LEN, 1024], F32)
    g0 = _dram_in(nc, "g0", [1, D], F32)
    w0 = _dram_in(nc, "w0", [D, 4096], F32)
    lam = _dram_in(nc, "lam", [1, 512], F32)
    subln = _dram_in(nc, "subln", [1, 256], F32)
    wo0 = _dram_in(nc, "wo0", [D, 1024], F32)
    gc = _dram_in(nc, "gc", [1, 1024], F32)
    w1 = _dram_in(nc, "w1", [D, 4096], F32)
    wo1 = _dram_in(nc, "wo1", [D, 1024], F32)
    gf = _dram_in(nc, "gf", [1, 1024], F32)
    tabs = _tab_aps(nc)
    out = _dram_out(nc, "out", [S_LEN, 1024], F32)
    def mk_pair(tag):
        ib = Rows([nc.dram_tensor("ib%s_%d" % (tag, i), [512, S_LEN], BF16).ap() for i in range(2)], 512)
        G = Rows([nc.dram_tensor("G%s_%d" % (tag, i), [1024, S_LEN], BF16).ap() for i in range(2)], 1024)
        return ib, G
    ib1, G1 = mk_pair("1")
    ib2 = [nc.dram_tensor("ib2_%d" % i, [1024, 1024], BF16).ap() for i in range(2)]
    G2 = [nc.dram_tensor("G2_%d" % i, [2048, 1024], BF16).ap() for i in range(2)]
    ib3, G3 = mk_pair("3")
    hd = nc.dram_tensor("hd", [S_LEN, 1024], F32).ap()
    ssi = [nc.dram_tensor("ssi%d" % i, [128, 16], F32).ap() for i in range(2)]
    sso = [nc.dram_tensor("sso%d" % i, [256, 16], F32).ap() for i in range(2)]
    with ExitStack() as st:
        K = _mk_ctx(nc, st)
        K.tabs = tabs
        _common(K, st)
        mixer_phase(K, "ab", xin, g0, w0, ib1, tabs, dict(lam=lam, subln=subln), ag=(ib1, G1))
        outproj_cols(K, G1, wo0, xres, gc, False, (ib2, G2), hd, ssi[0], sso[0], pre_cc=[(ib1.pieces[1], G1.pieces[1])])
        mixer_phase(K, "c", None, None, w1, ib3, tabs, {}, xnT_src=G2, ag=(ib3, G3), pre_cc=[(ib2[1], G2[1])])
        outproj_cols(K, G3, wo1, hd, gf, True, out, None, ssi[1], sso[1], pre_cc=[(ib3.pieces[1], G3.pieces[1])])
        K.S.emit()
    return nc


def _common(K, st):
    K.epsb = T_(K, st, "epsb", [128, 1], F32)
    K.r_epsb = Res("epsb")
    K.S.op("dve" if "B" in DBG else "pool", lambda e: e.memset(K.epsb[:], EPS), writes=[K.r_epsb])


def _dram_in(nc, name, shape, dt):
    return nc.dram_tensor(name, list(shape), dt, kind="ExternalInput").ap()


def _dram_out(nc, name, shape, dt):
    return nc.dram_tensor(name, list(shape), dt, kind="ExternalOutput").ap()


def _tab_aps(nc):
    return dict(
        cos=_dram_in(nc, "cos", [S_LEN, 64], F32), sin=_dram_in(nc, "sin", [S_LEN, 64], F32),
        maskA=_dram_in(nc, "maskA", [128, 9, 512], BF16), maskC=_dram_in(nc, "maskC", [128, 4, 512], BF16),
        Emat=_dram_in(nc, "Emat", [128, 8, 128], BF16), ident=_dram_in(nc, "ident", [128, 128], BF16))


def build_mixer(kind, stop=None):
    nc = bass.Bass("TRN2", target_bir_lowering=False)
    xin = _dram_in(nc, "xin", [S_LEN, D], F32)
    g = _dram_in(nc, "g", [1, D], F32)
    w = _dram_in(nc, "w", [D, 4096], F32)
    tabs = _tab_aps(nc)
    extra = {}
    if kind == "ab":
        extra["lam"] = _dram_in(nc, "lam", [1, 512], F32)
        extra["subln"] = _dram_in(nc, "subln", [1, 256], F32)
    yT = _dram_out(nc, "yT", [1024, S_LEN], BF16)
    with ExitStack() as st:
        K = _mk_ctx(nc, st)
        K.stop = stop
        _common(K, st)
        mixer_phase(K, kind, xin, g, w, yT, tabs, extra)
        K.S.emit()
    return nc


def build_outproj(final):
    nc = bass.Bass("TRN2", target_bir_lowering=False)
    yT = _dram_in(nc, "yTin", [D, 1024], BF16)
    res = _dram_in(nc, "res", [1024, D], F32)
    w = _dram_in(nc, "w", [D, D], F32)
    g = _dram_in(nc, "g", [1, D], F32) if final else None
    out = _dram_out(nc, "out", [1024, D], F32)
    with ExitStack() as st:
        K = _mk_ctx(nc, st)
        _common(K, st)
        outproj_phase(K, final, yT, res, w, out, None, g)
        K.S.emit()
    return nc


def _cols_ab(hh):
    r = lambda a, n: list(range(a, a + n))
    o = hh * 512
    return (r(0 + o, 512) + r(1024 + o, 512) + r(2048 + o, 512) + r(6144 + o, 512)
            + r(3072 + o, 512) + r(4096 + o, 512) + r(5120 + o, 512) + r(6144 + 1024 + o, 512))


def _cols_c(hh):
    r = lambda a, n: list(range(a, a + n))
    cols = []
    for rnd in range(2):
        o = hh * 1024 + rnd * 512
        cols += r(o, 512) + r(2048 + o, 512) + r(4096 + o, 512) + r(6144 + o, 512)
    return cols


_PERM_AB = list(range(0, 512)) + list(range(1024, 1536)) + list(range(512, 1024)) + list(range(1536, 2048))

_CACHE = {}


def _get(name, fn):
    if name not in _CACHE:
        _CACHE[name] = fn()
    return _CACHE[name]


def kernel_unfused(x, norm_ab, w_in_ab, w_out_ab, lam_ab, subln_ab, norm_c, w_in_c, w_out_c, final_norm):
    f32 = np.float32
    x = np.asarray(x, f32)
    tabs = _get("tabs", _tables)
    cores = list(range(8))
    tabin = dict(cos=tabs["cos"], sin=tabs["sin"], maskA=tabs["maskA"], maskC=tabs["maskC"], Emat=tabs["Emat"], ident=tabs["ident"])

    w0 = np.asarray(w_in_ab, f32)[0]
    wab = [np.ascontiguousarray(w0[:, _cols_ab(hh)]) for hh in range(2)]
    nc1 = _get("mix_ab", lambda: build_mixer("ab"))
    in1 = []
    for c in cores:
        b, hh = c // 2, c % 2
        d = dict(xin=x[b], g=np.asarray(norm_ab, f32)[0:1], w=wab[hh], lam=np.asarray(lam_ab, f32)[0].reshape(1, 512),
                 subln=np.asarray(subln_ab, f32)[0:1])
        d.update(tabin)
        in1.append(d)
    r1 = run_bass_kernel_spmd(nc1, in1, core_ids=cores).results
    yT0 = [np.concatenate([r1[2 * b]["yT"], r1[2 * b + 1]["yT"]], axis=0) for b in range(4)]

    wo0 = np.ascontiguousarray(np.asarray(w_out_ab, f32)[0][_PERM_AB, :])
    nc2 = _get("op0", lambda: build_outproj(False))
    in2 = []
    for c in cores:
        b, hh = c // 2, c % 2
        in2.append(dict(yTin=np.ascontiguousarray(yT0[b][:, hh * 1024:(hh + 1) * 1024]), res=x[b, hh * 1024:(hh + 1) * 1024], w=wo0))
    r2 = run_bass_kernel_spmd(nc2, in2, core_ids=cores).results
    h0 = [np.concatenate([r2[2 * b]["out"], r2[2 * b + 1]["out"]], axis=0) for b in range(4)]

    w1 = np.asarray(w_in_c, f32)[0]
    wc = [np.ascontiguousarray(w1[:, _cols_c(hh)]) for hh in range(2)]
    nc3 = _get("mix_c", lambda: build_mixer("c"))
    in3 = []
    for c in cores:
        b, hh = c // 2, c % 2
        d = dict(xin=h0[b], g=np.asarray(norm_c, f32)[0:1], w=wc[hh])
        d.update(tabin)
        in3.append(d)
    r3 = run_bass_kernel_spmd(nc3, in3, core_ids=cores).results
    yT1 = [np.concatenate([r3[2 * b]["yT"], r3[2 * b + 1]["yT"]], axis=0) for b in range(4)]

    wo1 = np.ascontiguousarray(np.asarray(w_out_c, f32)[0])
    nc4 = _get("op1", lambda: build_outproj(True))
    in4 = []
    for c in cores:
        b, hh = c // 2, c % 2
        in4.append(dict(yTin=np.ascontiguousarray(yT1[b][:, hh * 1024:(hh + 1) * 1024]), res=np.ascontiguousarray(h0[b][hh * 1024:(hh + 1) * 1024]),
                        w=wo1, g=np.asarray(final_norm, f32).reshape(1, D)))
    r4 = run_bass_kernel_spmd(nc4, in4, core_ids=cores).results
    out = np.stack([np.concatenate([r4[2 * b]["out"], r4[2 * b + 1]["out"]], axis=0) for b in range(4)], 0)
    return out.astype(f32)


def _fused_inputs(x, norm_ab, w_in_ab, w_out_ab, lam_ab, subln_ab, norm_c, w_in_c, w_out_c, final_norm):
    f32 = np.float32
    x = np.asarray(x, f32)
    tabs = _get("tabs", _tables)
    w0 = np.asarray(w_in_ab, f32)[0]
    w1 = np.asarray(w_in_c, f32)[0]
    wo0 = np.asarray(w_out_ab, f32)[0]
    wo1 = np.asarray(w_out_c, f32)[0][_PERM_AB, :]
    per_hh = []
    for hh in range(2):
        cs = slice(hh * 1024, (hh + 1) * 1024)
        per_hh.append(dict(
            w0=np.ascontiguousarray(w0[:, _cols_ab(hh)]), w1=np.ascontiguousarray(w1[:, _cols_c(hh)]),
            wo0=np.ascontiguousarray(wo0[:, cs]), wo1=np.ascontiguousarray(wo1[:, cs]),
            gc=np.ascontiguousarray(np.asarray(norm_c, f32)[0:1, cs]), gf=np.ascontiguousarray(np.asarray(final_norm, f32).reshape(1, D)[:, cs])))
    ins = []
    for c in range(8):
        b, hh = c // 2, c % 2
        d = dict(xin=x[b], xres=np.ascontiguousarray(x[b][:, hh * 1024:(hh + 1) * 1024]), g0=np.asarray(norm_ab, f32)[0:1],
                 lam=np.asarray(lam_ab, f32)[0].reshape(1, 512), subln=np.asarray(subln_ab, f32)[0:1],
                 cos=tabs["cos"], sin=tabs["sin"], maskA=tabs["maskA"], maskC=tabs["maskC"], Emat=tabs["Emat"], ident=tabs["ident"])
        d.update(per_hh[hh])
        ins.append(d)
    return ins


def kernel(x, norm_ab, w_in_ab, w_out_ab, lam_ab, subln_ab, norm_c, w_in_c, w_out_c, final_norm):
    ins = _fused_inputs(x, norm_ab, w_in_ab, w_out_ab, lam_ab, subln_ab, norm_c, w_in_c, w_out_c, final_norm)
    nc = _get("fused", build_fused)
    res = run_bass_kernel_spmd(nc, ins, core_ids=list(range(8))).results
    out = np.stack([np.concatenate([res[2 * b]["out"], res[2 * b + 1]["out"]], axis=1) for b in range(4)], 0)
    return out.astype(np.float32)
```
